# Optimizing a Trainium2 kernel written in Bass

```python
import math
import jax, jax.numpy as jnp
from jax import lax
import numpy as np

D_MODEL = 2048
BATCH = 8
SEQ = 4096
DEPTH = 4

GRID_W = 64
N_MIXERS = 2
HEAD_DIM = 128
A_HEADS = 16
A_KV_HEADS = 4
ROPE_THETA = 10000.0
A_Q_BLOCK = 128
A_QKV_WIDTH = (A_HEADS + 2 * A_KV_HEADS) * HEAD_DIM
B_GROUPS = ((128, 1), (512, 4), (2048, 16))
B_HEADS_PER_GROUP = 8
B_Q_BLOCK = 64
B_WIDTH = len(B_GROUPS) * B_HEADS_PER_GROUP * HEAD_DIM
B_QKV_WIDTH = 3 * B_WIDTH
REL_BUCKETS = 32
REL_MAX_DISTANCE = 1024
D_FF = 5632
CONV_WIDTH = 3
EPS = 1e-6
NEG_INF = -1e30
N_A_LAYERS = (DEPTH + 1) // 2
N_B_LAYERS = DEPTH // 2

kernel_name = "hybrid_axial_gqa_dilated_convffn_encoder"


def rms_norm(x, gain):
    xf = x.astype(jnp.float32)
    y = xf * lax.rsqrt(jnp.mean(xf * xf, axis=-1, keepdims=True) + EPS) * gain.astype(jnp.float32)
    return y.astype(x.dtype)


def axial_rope(x):
    seq = x.shape[1]
    rows = seq // GRID_W
    row_ids = jnp.repeat(jnp.arange(rows, dtype=jnp.float32), GRID_W)
    col_ids = jnp.tile(jnp.arange(GRID_W, dtype=jnp.float32), rows)
    half = HEAD_DIM // 2
    quarter = half // 2
    inv_freq = ROPE_THETA ** (-jnp.arange(quarter, dtype=jnp.float32) / quarter)

    def rot(xs, pos):
        ang = pos[:, None] * inv_freq[None, :]
        cos = jnp.cos(ang)[None, :, None, :]
        sin = jnp.sin(ang)[None, :, None, :]
        x1, x2 = xs[..., :quarter], xs[..., quarter:]
        return jnp.concatenate([x1 * cos - x2 * sin, x2 * cos + x1 * sin], axis=-1)

    return jnp.concatenate([rot(x[..., :half], row_ids), rot(x[..., half:], col_ids)], axis=-1)


def mixer_a(h, w_qkv, w_o, q_gain, k_gain):
    b, s, _ = h.shape
    qkv = h @ w_qkv
    nq = A_HEADS * HEAD_DIM
    nk = A_KV_HEADS * HEAD_DIM
    q = qkv[..., :nq].reshape(b, s, A_HEADS, HEAD_DIM)
    k = qkv[..., nq:nq + nk].reshape(b, s, A_KV_HEADS, HEAD_DIM)
    v = qkv[..., nq + nk:].reshape(b, s, A_KV_HEADS, HEAD_DIM)
    q = axial_rope(rms_norm(q.astype(jnp.float32), q_gain)).astype(h.dtype)
    k = axial_rope(rms_norm(k.astype(jnp.float32), k_gain)).astype(h.dtype)
    grp = A_HEADS // A_KV_HEADS
    nblk = s // A_Q_BLOCK
    qb = q.reshape(b, nblk, A_Q_BLOCK, A_KV_HEADS, grp, HEAD_DIM).transpose(1, 0, 2, 3, 4, 5)
    scale = HEAD_DIM ** -0.5

    def attend(q_blk):
        logits = jnp.einsum('bqkgd,bskd->bkgqs', q_blk, k).astype(jnp.float32) * scale
        p = jax.nn.softmax(logits, axis=-1).astype(v.dtype)
        return jnp.einsum('bkgqs,bskd->bqkgd', p, v)

    o = lax.map(attend, qb)
    o = o.transpose(1, 0, 2, 3, 4, 5).reshape(b, s, nq)
    return o @ w_o


def t5_bucket(rel):
    nb = REL_BUCKETS // 2
    max_exact = nb // 2
    base = jnp.where(rel > 0, nb, 0)
    n = jnp.abs(rel)
    nf = jnp.maximum(n, 1).astype(jnp.float32)
    large = max_exact + (jnp.log(nf / max_exact) / math.log(REL_MAX_DISTANCE / max_exact)
                         * (nb - max_exact)).astype(jnp.int32)
    large = jnp.minimum(large, nb - 1)
    return base + jnp.where(n < max_exact, n, large)


def dilated_group(q, k, v, rel_bias_g, window, dilation):
    b, s, h, d = q.shape
    half_span = window // (2 * dilation)
    L = s // dilation
    nblk = -(-L // B_Q_BLOCK)
    Lp = nblk * B_Q_BLOCK
    kv_len = B_Q_BLOCK + 2 * half_span
    qs = q.reshape(b, L, dilation, h, d)
    ks = k.reshape(b, L, dilation, h, d)
    vs = v.reshape(b, L, dilation, h, d)
    qs = jnp.pad(qs, ((0, 0), (0, Lp - L), (0, 0), (0, 0), (0, 0)))
    pad_kv = ((0, 0), (half_span, Lp - L + half_span), (0, 0), (0, 0), (0, 0))
    ks = jnp.pad(ks, pad_kv)
    vs = jnp.pad(vs, pad_kv)
    blk_start = jnp.arange(nblk) * B_Q_BLOCK
    key_idx = blk_start[:, None] + jnp.arange(kv_len)[None, :]
    kb = ks[:, key_idx]
    vb = vs[:, key_idx]
    qb = qs.reshape(b, nblk, B_Q_BLOCK, dilation, h, d)
    rel = jnp.arange(kv_len)[None, :] - half_span - jnp.arange(B_Q_BLOCK)[:, None]
    bias = rel_bias_g[t5_bucket(rel * dilation)].astype(jnp.float32).transpose(2, 0, 1)
    key_pos = key_idx - half_span
    valid = (jnp.abs(rel) <= half_span)[None] & ((key_pos >= 0) & (key_pos < L))[:, None, :]
    scale = d ** -0.5
    logits = jnp.einsum('bnqchd,bnkchd->bnchqk', qb, kb).astype(jnp.float32) * scale + bias[None, None, None]
    logits = jnp.where(valid[None, :, None, None], logits, NEG_INF)
    m = jnp.max(logits, axis=-1, keepdims=True)
    p = jnp.exp(logits - m)
    l = jnp.sum(p, axis=-1, keepdims=True)
    o = jnp.einsum('bnchqk,bnkchd->bnqchd', p.astype(v.dtype), vb).astype(jnp.float32)
    l_t = l[..., 0].transpose(0, 1, 4, 2, 3)
    o = o / l_t[..., None]
    log_z = (m[..., 0] + jnp.log(l[..., 0])).transpose(0, 1, 4, 2, 3)
    o = o.reshape(b, Lp, dilation, h, d)[:, :L].reshape(b, s, h, d)
    log_z = log_z.reshape(b, Lp, dilation, h)[:, :L].reshape(b, s, h)
    return o, log_z


def mixer_b(h, w_qkv, w_o, rel_bias):
    b, s, _ = h.shape
    n_g = len(B_GROUPS)
    hg = B_HEADS_PER_GROUP
    qkv = (h @ w_qkv).reshape(b, s, n_g, 3, hg, HEAD_DIM)
    outs, log_zs = [], []
    for g, (window, dil) in enumerate(B_GROUPS):
        o, lz = dilated_group(qkv[:, :, g, 0], qkv[:, :, g, 1], qkv[:, :, g, 2],
                              rel_bias[:, g * hg:(g + 1) * hg], window, dil)
        outs.append(o)
        log_zs.append(lz)
    wts = jax.nn.softmax(jnp.stack(log_zs, axis=0), axis=0)
    y = jnp.concatenate([wts[g][..., None] * outs[g] for g in range(n_g)], axis=2)
    y = y.reshape(b, s, B_WIDTH).astype(h.dtype)
    return y @ w_o


def conv_ffn(h, w_up, conv_w, conv_b, w_down):
    u = h @ w_up
    c = u.shape[-1]
    pad = CONV_WIDTH // 2
    u = lax.conv_general_dilated(u, conv_w[:, None, :].astype(u.dtype), window_strides=(1,),
                                 padding=((pad, pad),), dimension_numbers=('NWC', 'WIO', 'NWC'),
                                 feature_group_count=c) + conv_b
    gate, val = jnp.split(u, 2, axis=-1)
    return (jax.nn.silu(gate) * val) @ w_down


def setup_inputs(seed: int = 0) -> dict:
    key = jax.random.key(seed)
    ks = jax.random.split(key, 16)
    f32 = jnp.float32
    nrm = lambda k, shape, sc: jax.random.normal(k, shape, f32) * sc
    return {
        "x": jax.random.normal(ks[0], (BATCH, SEQ, D_MODEL), f32),
        "a_w_qkv": nrm(ks[1], (N_A_LAYERS, D_MODEL, A_QKV_WIDTH), D_MODEL ** -0.5),
        "a_w_o": nrm(ks[2], (N_A_LAYERS, A_HEADS * HEAD_DIM, D_MODEL), (A_HEADS * HEAD_DIM) ** -0.5),
        "a_q_gain": 1.0 + nrm(ks[3], (N_A_LAYERS, HEAD_DIM), 0.02),
        "a_k_gain": 1.0 + nrm(ks[4], (N_A_LAYERS, HEAD_DIM), 0.02),
        "b_w_qkv": nrm(ks[5], (N_B_LAYERS, D_MODEL, B_QKV_WIDTH), D_MODEL ** -0.5),
        "b_w_o": nrm(ks[6], (N_B_LAYERS, B_WIDTH, D_MODEL), B_WIDTH ** -0.5),
        "rel_bias": nrm(ks[7], (REL_BUCKETS, len(B_GROUPS) * B_HEADS_PER_GROUP), 0.5),
        "mix_norm": 1.0 + nrm(ks[8], (DEPTH, D_MODEL), 0.02),
        "ffn_norm": 1.0 + nrm(ks[9], (DEPTH, D_MODEL), 0.02),
        "w_up": nrm(ks[10], (DEPTH, D_MODEL, 2 * D_FF), D_MODEL ** -0.5),
        "conv_w": nrm(ks[11], (DEPTH, CONV_WIDTH, 2 * D_FF), CONV_WIDTH ** -0.5),
        "conv_b": nrm(ks[12], (DEPTH, 2 * D_FF), 0.01),
        "w_down": nrm(ks[13], (DEPTH, D_FF, D_MODEL), D_FF ** -0.5),
        "final_norm": 1.0 + nrm(ks[14], (D_MODEL,), 0.02),
    }


def reference(x, a_w_qkv, a_w_o, a_q_gain, a_k_gain, b_w_qkv, b_w_o, rel_bias,
              mix_norm, ffn_norm, w_up, conv_w, conv_b, w_down, final_norm):
    h = x
    for i in range(DEPTH):
        hn = rms_norm(h, mix_norm[i])
        j = i // N_MIXERS
        if i % N_MIXERS == 0:
            h = h + mixer_a(hn, a_w_qkv[j], a_w_o[j], a_q_gain[j], a_k_gain[j])
        else:
            h = h + mixer_b(hn, b_w_qkv[j], b_w_o[j], rel_bias)
        h = h + conv_ffn(rms_norm(h, ffn_norm[i]), w_up[i], conv_w[i], conv_b[i], w_down[i])
    return rms_norm(h, final_norm)
```

```python
import numpy as np
import ml_dtypes
from contextlib import ExitStack
import concourse.bass as bass
import concourse.mybir as mybir
from concourse.bass_utils import run_bass_kernel_spmd

F32 = mybir.dt.float32
BF16 = mybir.dt.bfloat16
AF = mybir.ActivationFunctionType
ALU = mybir.AluOpType

S = 4096
D = 2048
DFF = 5632
NFT = 44
HD = 128
DEPTH = 4
EPS = 1e-6
NEG = -30000.0
B_GROUPS = ((128, 1), (512, 4), (2048, 16))
SB_BASE = 16512
SB_END = 229376
DT_SIZE = {F32: 4, BF16: 2}


class Sem:
    def __init__(self, h):
        self.h = h
        self.n = 0


class Buf:
    __slots__ = ("w", "r")

    def __init__(self):
        self.w = {}
        self.r = {}


class Prog:
    ENGS = ("sync", "scalar", "vector", "gpsimd", "tensor")
    DMAE = ("sync", "gpsimd", "scalar")
    NDS = 12

    def __init__(self, nc, es):
        self.nc, self.es = nc, es
        self.q = {e: [] for e in self.ENGS}
        self.esem = {e: self.newsem("s_" + e) for e in self.ENGS}
        self.dsem = {e: [self.newsem(f"d_{e}{i}") for i in range(self.NDS)] for e in self.DMAE}
        self.dctr = {e: 0 for e in self.DMAE}
        self.bar = self.newsem("bar")
        self.nbar = 0
        self.known = {e: {} for e in self.ENGS}
        self.sb_off = SB_BASE
        self.uid = 0
        self.nops = 0

    def newsem(self, name):
        return Sem(self.es.enter_context(self.nc.semaphore(name)))

    def alloc(self, name, shape, dtype):
        n = DT_SIZE[dtype]
        for s in shape[1:]:
            n *= s
        off = (self.sb_off + 63) // 64 * 64
        assert off + n <= SB_END, (name, off, n)
        self.sb_off = off + n
        self.uid += 1
        return self.nc.alloc_sbuf_tensor_at(f"{name}_{self.uid}", list(shape), dtype, offset=off)

    def _waits(self, eng, reads, writes, is_dma, merge=False):
        need = {}
        for b in reads:
            for s, v in b.w.items():
                if need.get(s, 0) < v:
                    need[s] = v
        for b in writes:
            if not merge:
                for s, v in b.w.items():
                    if need.get(s, 0) < v:
                        need[s] = v
            for s, v in b.r.items():
                if need.get(s, 0) < v:
                    need[s] = v
        kn = self.known[eng]
        own = self.esem[eng]
        out = []
        for s, v in need.items():
            if s is own and not is_dma:
                continue
            if kn.get(s, 0) >= v:
                continue
            kn[s] = v
            out.append((s, v))
        return out

    def _book(self, s, val, reads, writes, merge=False):
        for b in reads:
            b.r[s] = val
        for b in writes:
            if merge:
                b.w[s] = val
            else:
                b.w = {s: val}
                b.r = {}

    def op(self, eng, fn, reads=(), writes=(), signal=True, merge=False, own=False):
        waits = self._waits(eng, reads, writes, own, merge)
        self.nops += 1
        if signal:
            s = self.esem[eng]
            s.n += 1
            self._book(s, s.n, reads, writes, merge)
            self.q[eng].append((fn, waits, s, 1))
        else:
            self.q[eng].append((fn, waits, None, 0))

    def dma(self, eng, out, in_, reads=(), writes=(), merge=False):
        k = self.dctr[eng] % self.NDS
        self.dctr[eng] += 1
        s = self.dsem[eng][k]
        waits = self._waits(eng, reads, writes, True, merge)
        kn = self.known[eng]
        if s.n > 0 and kn.get(s, 0) < s.n:
            waits.append((s, s.n))
            kn[s] = s.n
        s.n += 16
        self._book(s, s.n, reads, writes, merge)
        self.nops += 1
        self.q[eng].append((lambda e: e.dma_start(out=out, in_=in_), waits, s, 16))

    def barrier(self):
        for e in self.DMAE:
            waits = []
            for s in self.dsem[e]:
                if s.n > self.known[e].get(s, 0):
                    waits.append((s, s.n))
            if waits:
                self.q[e].append((None, waits, None, 0))
        self.nbar += 1
        tgt = 5 * self.nbar
        for e in self.ENGS:
            self.q[e].append((lambda en: en.drain(), [], self.bar, 1))
            self.q[e].append((None, [(self.bar, tgt)], None, 0))
        allsems = list(self.esem.values()) + [s for e in self.DMAE for s in self.dsem[e]]
        for e in self.ENGS:
            for s in allsems:
                self.known[e][s] = s.n

    def flush(self, block):
        for name in self.ENGS:
            ops = self.q[name]

            def body(e, ops=ops):
                for fn, waits, s, amt in ops:
                    for ws, wv in waits:
                        e.wait_ge(ws.h, wv)
                    if fn is None:
                        continue
                    ins = fn(e)
                    if s is not None:
                        ins.then_inc(s.h, amt)

            getattr(block, name)(body)


def mm(p, out, lhsT, rhs, start, stop, reads=(), writes=(), signal=False):
    p.op("tensor", lambda e: e.matmul(out, lhsT=lhsT, rhs=rhs, start=start, stop=stop), reads, writes, signal)


def act(p, out, in_, func, reads=(), writes=(), scale=None, bias=None, eng="scalar"):
    kw = {}
    if scale is not None:
        kw["scale"] = scale
    if bias is not None:
        kw["bias"] = bias
    p.op("scalar", lambda e: e.activation(out=out, in_=in_, func=func, **kw), reads, writes)


def tt(p, eng, out, in0, in1, op, reads=(), writes=()):
    p.op(eng, lambda e: e.tensor_tensor(out=out, in0=in0, in1=in1, op=op), reads, writes)


def stt(p, out, in0, scalar, in1, op0, op1, reads=(), writes=()):
    p.op("vector", lambda e: e.scalar_tensor_tensor(out=out, in0=in0, scalar=scalar, in1=in1, op0=op0, op1=op1),
         reads, writes)


def cp(p, eng, out, in_, reads=(), writes=(), merge=False):
    if eng == "scalar":
        p.op("scalar", lambda e: e.activation(out=out, in_=in_, func=AF.Copy), reads, writes, merge=merge)
    else:
        p.op(eng, lambda e: e.tensor_copy(out=out, in_=in_), reads, writes, merge=merge)


class Ctx:
    pass


class WLoader:
    PIECE = 1024

    def __init__(self, p, nstage=4):
        self.p = p
        self.st = [p.alloc("wst", [128, self.PIECE], F32) for _ in range(nstage)]
        self.stb = [Buf() for _ in range(nstage)]
        self.n = 0

    def load(self, dst, src, K, N, dstbuf):
        p = self.p
        kk = max(1, self.PIECE // N)
        for k0 in range(0, K, kk):
            k1 = min(K, k0 + kk)
            i = self.n % len(self.st)
            eng = "gpsimd" if (self.n % 2 == 0) else "vector"
            self.n += 1
            stv = self.st[i][:, 0:(k1 - k0) * N].rearrange("p (k n) -> p k n", n=N)
            p.dma("sync", stv, src[:, k0:k1, :], writes=[self.stb[i]])
            p.op(eng, lambda e, o=dst[:, k0:k1, :], i_=stv: e.tensor_copy(out=o, in_=i_),
                 reads=[self.stb[i]], writes=[dstbuf], merge=True)


def phase_x_to_hT(p, c):
    base = p.sb_off
    xs = [p.alloc("xs", [128, D], F32) for _ in range(2)]
    xsb = [Buf() for _ in range(2)]
    stg = [p.alloc("stg", [128, 16, 512], F32) for _ in range(2)]
    stgb = [Buf() for _ in range(2)]
    hTv = c.hT.rearrange("(c p) t -> p c t", p=128)
    bk = 0
    for g in range(8):
        sl = g % 2
        for sub in range(4):
            t = g * 4 + sub
            xsl = t % 2
            p.dma("sync", xs[xsl][:, :], c.x[t * 128:(t + 1) * 128, :], writes=[xsb[xsl]])
            for c4 in range(4):
                b = bk % 8
                bk += 1
                for j in range(4):
                    cc = c4 * 4 + j
                    o = c.ps[:, b, j * 128:(j + 1) * 128]
                    i = xs[xsl][:, cc * 128:(cc + 1) * 128]
                    p.op("tensor", lambda e, o=o, i=i: e.transpose(o, i, c.ident_f[:, :]),
                         reads=[xsb[xsl]], writes=[c.psb[b]], signal=(j == 3))
                o = stg[sl][:, c4 * 4:(c4 + 1) * 4, sub * 128:(sub + 1) * 128]
                i = c.ps[:, b, :].rearrange("p (j t) -> p j t", j=4)
                cp(p, "vector" if (c4 % 2) else "scalar", o, i, reads=[c.psb[b]], writes=[stgb[sl]], merge=True)
        p.dma("gpsimd", hTv[:, :, g * 512:(g + 1) * 512], stg[sl][:, :, :], reads=[stgb[sl]])
    p.barrier()
    p.sb_off = base


def emit_norm(p, c, gain, xn, dt_out, nt0=0, nt1=8):
    base = p.sb_off
    TW = 256
    hs = [p.alloc("hs", [128, 16, TW], F32) for _ in range(3)]
    hsb = [Buf() for _ in range(3)]
    sq = [p.alloc("sq", [128, 16, TW], BF16) for _ in range(2)]
    sqb = [Buf() for _ in range(2)]
    rs = [p.alloc("rs", [128, TW], F32) for _ in range(2)]
    rsb = [Buf() for _ in range(2)]
    xnb = Buf()
    hTv = c.hT.rearrange("(c p) t -> p c t", p=128)
    r = 512 // TW
    for j in range(nt0 * r, nt1 * r):
        sl = j % 2
        hl = j % 3
        p.dma("sync", hs[hl][:, :, :], hTv[:, :, j * TW:(j + 1) * TW], writes=[hsb[hl]])
        b = c.nbk % 8
        c.nbk += 1
        act(p, sq[sl][:, :, :], hs[hl][:, :, :], AF.Square, reads=[hsb[hl]], writes=[sqb[sl]])
        for k in range(16):
            mm(p, c.ps[:, b, 0:TW], c.ones_db[:, :], sq[sl][:, k, :], k == 0, k == 15,
               reads=[sqb[sl]], writes=[c.psb[b]], signal=(k == 15))
        act(p, rs[sl][:, :], c.ps[:, b, 0:TW], AF.Sqrt, reads=[c.psb[b]], writes=[rsb[sl]], bias=c.eps_t[:, 0:1])
        p.op("vector", lambda e, o=rs[sl][:, :]: e.reciprocal(out=o, in_=o), reads=[rsb[sl]], writes=[rsb[sl]])
        jj = j - nt0 * r
        for k in range(16):
            stt(p, xn[:, k, jj * TW:(jj + 1) * TW], hs[hl][:, k, :], gain[:, k:k + 1], rs[sl][:, :],
                ALU.mult, ALU.mult, reads=[hsb[hl], rsb[sl]], writes=[xnb])
    p.barrier()
    p.sb_off = base


def proj_fm(p, c, xn, wfn, nslab, slab_w, evac, nk=16, ntt=8):
    ws = [p.alloc("ws", [128, nk, slab_w], BF16) for _ in range(2)]
    wsb = [Buf() for _ in range(2)]
    wl = WLoader(p)

    def lw(si):
        wl.load(ws[si % 2][:, :, :], wfn(si).rearrange("(k p) n -> p k n", p=128), nk, slab_w, wsb[si % 2])

    lw(0)
    for si in range(nslab):
        sl = si % 2
        if si + 1 < nslab:
            lw(si + 1)
        for m in range(slab_w // 128):
            for t in range(ntt):
                b = c.nbk % c.nproj_banks
                c.nbk += 1
                for k in range(nk):
                    mm(p, c.ps[:, b, :], ws[sl][:, k, m * 128:(m + 1) * 128], xn[:, k, t * 512:(t + 1) * 512],
                       k == 0, k == nk - 1, reads=[wsb[sl]], writes=[c.psb[b]], signal=(k == nk - 1))
                evac(si, m, t, b)


def proj_tm(p, c, xn, wfn, nslab, store, nk=16):
    ws = [p.alloc("wv", [128, nk, 512], BF16) for _ in range(2)]
    wsb = [Buf() for _ in range(2)]
    st = [p.alloc("vst", [128, 4, 512], BF16) for _ in range(2)]
    stb = [Buf() for _ in range(2)]
    n = 0
    wl = WLoader(p)

    def lw(si):
        wl.load(ws[si % 2][:, :, :], wfn(si).rearrange("(k p) n -> p k n", p=128), nk, 512, wsb[si % 2])

    lw(0)
    for si in range(nslab):
        sl = si % 2
        if si + 1 < nslab:
            lw(si + 1)
        for t4 in range(8):
            ssl = n % 2
            n += 1
            for tq in range(4):
                t = t4 * 4 + tq
                b = c.nbk % c.nproj_banks
                c.nbk += 1
                for k in range(nk):
                    mm(p, c.ps[:, b, :], xn[:, k, t * 128:(t + 1) * 128], ws[sl][:, k, :],
                       k == 0, k == nk - 1, reads=[wsb[sl]], writes=[c.psb[b]], signal=(k == nk - 1))
                cp(p, "vector" if (tq % 2) else "scalar", st[ssl][:, tq, :], c.ps[:, b, :],
                   reads=[c.psb[b]], writes=[stb[ssl]], merge=True)
            store(si, t4, st[ssl], stb[ssl])


def phase_A1(p, c, j):
    base = p.sb_off
    xn = p.alloc("xn", [128, 16, S], BF16)
    emit_norm(p, c, c.mixg[:, c.layer * 16:(c.layer + 1) * 16], xn, BF16)
    base2 = p.sb_off
    cst = [p.alloc("cst", [128, 2, 512], F32) for _ in range(2)]
    cstb = [Buf() for _ in range(2)]
    NW = 2
    qg = [p.alloc("qg", [128, 512], F32) for _ in range(NW)]
    sq = [p.alloc("sq", [128, 512], BF16) for _ in range(NW)]
    rs = [p.alloc("rs", [128, 512], F32) for _ in range(NW)]
    t1 = [p.alloc("t1", [128, 512], F32) for _ in range(NW)]
    t2 = [p.alloc("t2", [128, 512], F32) for _ in range(NW)]
    qgb = [Buf() for _ in range(NW)]
    sqb = [Buf() for _ in range(NW)]
    rsb = [Buf() for _ in range(NW)]
    t1b = [Buf() for _ in range(NW)]
    t2b = [Buf() for _ in range(NW)]
    ost = [p.alloc("ost", [128, 512], BF16) for _ in range(3)]
    ostb = [Buf() for _ in range(3)]
    wq = c.W("a_w_qkv")[j]
    cnt = [0]
    c.nproj_banks = 4

    def evac(si, m, t, b):
        head = si * 2 + m
        isq = head < 16
        gcol = c.aqg[:, j:j + 1] if isq else c.akg[:, j:j + 1]
        w = cnt[0] % NW
        cs = cnt[0] % 2
        cnt[0] += 1
        osl = head % 2
        p.dma("sync", cst[cs][:, 0, :], c.cos_d[:, t * 512:(t + 1) * 512], writes=[cstb[cs]], merge=True)
        p.dma("sync", cst[cs][:, 1, :], c.sin_d[:, t * 512:(t + 1) * 512], writes=[cstb[cs]], merge=True)
        bs = 4 + (cnt[0] % 2) * 2
        act(p, qg[w][:, :], c.ps[:, b, :], AF.Identity, reads=[c.psb[b]], writes=[qgb[w]], scale=gcol)
        act(p, sq[w][:, :], c.ps[:, b, :], AF.Square, reads=[c.psb[b]], writes=[sqb[w]])
        mm(p, c.ps[:, bs, :], c.ones_hb[:, :], sq[w][:, :], True, True, reads=[sqb[w]], writes=[c.psb[bs]], signal=True)
        mm(p, c.ps[:, bs + 1, :], c.perm[:, :], qg[w][:, :], True, True, reads=[qgb[w]], writes=[c.psb[bs + 1]],
           signal=True)
        act(p, rs[w][:, :], c.ps[:, bs, :], AF.Sqrt, reads=[c.psb[bs]], writes=[rsb[w]], bias=c.eps_t[:, 0:1])
        p.op("vector", lambda e, o=rs[w][:, :]: e.reciprocal(out=o, in_=o), reads=[rsb[w]], writes=[rsb[w]])
        tsl = slice(t * 512, (t + 1) * 512)
        tt(p, "vector", t1[w][:, :], qg[w][:, :], cst[cs][:, 0, :], ALU.mult, reads=[qgb[w], cstb[cs]],
           writes=[t1b[w]])
        tt(p, "vector", t2[w][:, :], c.ps[:, bs + 1, :], cst[cs][:, 1, :], ALU.mult, reads=[c.psb[bs + 1], cstb[cs]],
           writes=[t2b[w]])
        tt(p, "gpsimd", t1[w][:, :], t1[w][:, :], t2[w][:, :], ALU.add, reads=[t2b[w], t1b[w]], writes=[t1b[w]])
        osl = cnt[0] % 3
        tt(p, "vector", ost[osl][:, :], t1[w][:, :], rs[w][:, :], ALU.mult, reads=[t1b[w], rsb[w]],
           writes=[ostb[osl]])
        dst = c.qT[head * 128:(head + 1) * 128, tsl] if isq else c.kT[(head - 16) * 128:(head - 15) * 128, tsl]
        p.dma("gpsimd", dst, ost[osl][:, :], reads=[ostb[osl]])

    proj_fm(p, c, xn, lambda si: wq[:, si * 256:(si + 1) * 256], 10, 256, evac)
    p.barrier()
    p.sb_off = base2
    c.nproj_banks = 8

    def store(si, t4, st, stb):
        dst = c.v[t4 * 512:(t4 + 1) * 512, :].rearrange("(q p) n -> p q n", p=128)
        p.dma("gpsimd", dst, st[:, :, :], reads=[stb])

    proj_tm(p, c, xn, lambda si: wq[:, 2560:3072], 1, store)
    p.barrier()
    p.sb_off = base


def phase_A2(p, c):
    base = p.sb_off
    scale = float(HD) ** -0.5
    kT = [p.alloc("kT", [128, S], BF16) for _ in range(2)]
    vg = [p.alloc("vg", [128, 32, 128], BF16) for _ in range(2)]
    kvb = [Buf() for _ in range(2)]
    qh = [p.alloc("qh", [128, S], BF16) for _ in range(2)]
    qhb = [Buf() for _ in range(2)]
    NP = 4
    pT = [p.alloc("pT", [128, 512], BF16) for _ in range(NP)]
    pTb = [Buf() for _ in range(NP)]
    rz = [p.alloc("rz", [128, 512], F32) for _ in range(2)]
    rzb = [Buf() for _ in range(2)]
    ao = [p.alloc("ao", [128, S], BF16) for _ in range(2)]
    aob = [Buf() for _ in range(2)]
    ns = 0
    nq = 0
    npt = 0
    def lkv(g_):
        p.dma("sync", kT[g_ % 2][:, :], c.kT[g_ * 128:(g_ + 1) * 128, :], writes=[kvb[g_ % 2]], merge=True)
        p.dma("sync", vg[g_ % 2][:, :, :], c.v[:, g_ * 128:(g_ + 1) * 128].rearrange("(c p) d -> p c d", p=128),
              writes=[kvb[g_ % 2]], merge=True)

    def lq(h_):
        p.dma("sync", qh[h_ % 2][:, :], c.qT[h_ * 128:(h_ + 1) * 128, :], writes=[qhb[h_ % 2]])

    lkv(0)
    lq(0)
    for g in range(4):
        ksl = g % 2
        for hq in range(4):
            head = g * 4 + hq
            hsl = head % 2
            if head + 1 < 16:
                lq(head + 1)
            if hq == 3 and g + 1 < 4:
                lkv(g + 1)
            for qt in range(8):
                bo = 4 + (nq % 2)
                bz = 6 + (nq % 2)
                zsl = nq % 2
                nq += 1
                qsl = slice(qt * 512, (qt + 1) * 512)
                pend = []

                def qk(kc):
                    nonlocal ns
                    b = ns % 4
                    ns += 1
                    mm(p, c.ps[:, b, :], kT[ksl][:, kc * 128:(kc + 1) * 128], qh[hsl][:, qsl], True, True,
                       reads=[kvb[ksl], qhb[hsl]], writes=[c.psb[b]], signal=True)
                    return b

                def pv(kc, b):
                    nonlocal npt
                    w = npt % NP
                    npt += 1
                    act(p, pT[w][:, :], c.ps[:, b, :], AF.Exp, reads=[c.psb[b]], writes=[pTb[w]], scale=scale)
                    last = kc == 31
                    mm(p, c.ps[:, bo, :], vg[ksl][:, kc, :], pT[w][:, :], kc == 0, last,
                       reads=[pTb[w], kvb[ksl]], writes=[c.psb[bo]], signal=last)
                    mm(p, c.ps[:, bz, :], c.ones_b[:, :], pT[w][:, :], kc == 0, last,
                       reads=[pTb[w]], writes=[c.psb[bz]], signal=True)

                b0 = qk(0)
                b1 = qk(1)
                bl = [b0, b1]
                for kc in range(32):
                    if kc + 2 < 32:
                        bl.append(qk(kc + 2))
                    pv(kc, bl[kc])
                p.op("vector", lambda e, o=rz[zsl][:, :], i=c.ps[:, bz, :]: e.reciprocal(out=o, in_=i),
                     reads=[c.psb[bz]], writes=[rzb[zsl]])
                tt(p, "vector", ao[hsl][:, qsl], c.ps[:, bo, :], rz[zsl][:, :], ALU.mult,
                   reads=[c.psb[bo], rzb[zsl]], writes=[aob[hsl]])
            p.dma("gpsimd", c.aoT[head * 128:(head + 1) * 128, :], ao[hsl][:, :], reads=[aob[hsl]])
    p.barrier()
    p.sb_off = base


def phase_proj_res(p, c, actT, W, nk, TB, SW=512, tiled=False):
    base = p.sb_off
    at = p.alloc("at", [128, nk, TB], BF16)
    ws = [p.alloc("ws", [128, nk, SW], BF16) for _ in range(2)]
    wsb = [Buf() for _ in range(2)]
    wl = WLoader(p)
    hs = [p.alloc("hs", [128, TB], F32) for _ in range(2)]
    hsb = [Buf() for _ in range(2)]
    ho = [p.alloc("ho", [128, TB], F32) for _ in range(2)]
    hob = [Buf() for _ in range(2)]
    av = None if tiled else actT.rearrange("(k p) t -> p k t", p=128)
    wv = W.rearrange("(k p) n -> p k n", p=128)
    kstep = 4 if nk % 4 == 0 else nk
    atb = [Buf() for _ in range(nk // kstep)]
    NOS = D // SW
    NM = SW // 128
    nslab = (S // TB) * NOS

    def lw(i):
        os_ = i % NOS
        wl.load(ws[i % 2][:, :, :], wv[:, :, os_ * SW:(os_ + 1) * SW], nk, SW, wsb[i % 2])

    def lh(n_):
        tb_i, r = divmod(n_, 16)
        p.dma("sync", hs[n_ % 2][:, :], c.hT[r * 128:(r + 1) * 128, tb_i * TB:(tb_i + 1) * TB], writes=[hsb[n_ % 2]])

    lw(0)
    lh(0)
    n = 0
    for tb_i in range(S // TB):
        tsl = slice(tb_i * TB, (tb_i + 1) * TB)
        for k0 in range(0, nk, kstep):
            srcv = actT[tb_i, :, k0:k0 + kstep, :] if tiled else av[:, k0:k0 + kstep, tsl]
            p.dma("sync", at[:, k0:k0 + kstep, :], srcv, writes=[atb[k0 // kstep]])
        for os_ in range(NOS):
            i = tb_i * NOS + os_
            wsl = i % 2
            if i + 1 < nslab:
                lw(i + 1)
            for m in range(NM):
                ot = os_ * NM + m
                sl = n % 2
                if n + 1 < (S // TB) * 16:
                    lh(n + 1)
                n += 1
                for t in range(TB // 512):
                    b = c.nbk % 8
                    c.nbk += 1
                    for k in range(nk):
                        mm(p, c.ps[:, b, :], ws[wsl][:, k, m * 128:(m + 1) * 128], at[:, k, t * 512:(t + 1) * 512],
                           k == 0, k == nk - 1, reads=[wsb[wsl], atb[k // kstep]], writes=[c.psb[b]],
                           signal=(k == nk - 1 or (k % kstep) == kstep - 1))
                    tt(p, "vector", ho[sl][:, t * 512:(t + 1) * 512], c.ps[:, b, :], hs[sl][:, t * 512:(t + 1) * 512],
                       ALU.add, reads=[c.psb[b], hsb[sl]], writes=[hob[sl]])
                p.dma("scalar", c.hT[ot * 128:(ot + 1) * 128, tsl], ho[sl][:, :], reads=[hob[sl]])
    p.barrier()
    p.sb_off = base


def phase_F1(p, c, l):
    base = p.sb_off
    xn = p.alloc("xn", [128, 16, S], BF16)
    emit_norm(p, c, c.ffng[:, l * 16:(l + 1) * 16], xn, BF16)
    HT = 2048
    ws = [p.alloc("wu", [128, 16, 2, 128], BF16) for _ in range(2)]
    wsb = [Buf() for _ in range(2)]
    U = [p.alloc("U", [128, HT + 2], F32) for _ in range(2)]
    Ub = [Buf() for _ in range(2)]
    C = [p.alloc("C", [128, HT], F32) for _ in range(2)]
    Cb = [Buf() for _ in range(2)]
    A = [p.alloc("A", [128, HT], BF16) for _ in range(1)]
    Ab = [Buf() for _ in range(1)]
    wl = WLoader(p)
    wup = c.W("w_up")[l]
    wview = wup.rearrange("(k p) (g f) -> p k g f", p=128, g=2)
    n = 0
    na = 0

    def lwf(n_):
        fi_ = n_ % NFT
        for g_ in range(2):
            wl.load(ws[n_ % 2][:, :, g_, :], wview[:, :, g_, fi_ * 128:(fi_ + 1) * 128], 16, 128, wsb[n_ % 2])

    for half in range(2):
        t0 = half * HT
        for g in range(2):
            zc = 0 if half == 0 else HT + 1
            p.op("gpsimd", lambda e, o=U[g][:, zc:zc + 1]: e.memset(o, 0.0), writes=[Ub[g]])
        for fi in range(NFT):
            sl = n % 2
            if n == 0:
                lwf(0)
            if n + 1 < 2 * NFT:
                lwf(n + 1)
            n += 1
            cw = lambda g, j: c.convw[:, ((l * 2 + g) * 3 + j) * NFT + fi:((l * 2 + g) * 3 + j) * NFT + fi + 1]
            cb = lambda g: c.convb[:, (l * 2 + g) * NFT + fi:(l * 2 + g) * NFT + fi + 1]
            for g in range(2):
                hcol = HT + 1 if half == 0 else 0
                htok = t0 + HT if half == 0 else t0 - 1
                for k in range(16):
                    mm(p, c.ps[:, 7, 0:1], ws[sl][:, k, g, :], xn[:, k, htok:htok + 1], k == 0, k == 15,
                       reads=[wsb[sl]], writes=[c.psb[7]], signal=(k == 15))
                cp(p, "scalar", U[g][:, hcol:hcol + 1], c.ps[:, 7, 0:1], reads=[c.psb[7]], writes=[Ub[g]])
                for t in range(4):
                    b = c.nbk % 7
                    c.nbk += 1
                    for k in range(16):
                        mm(p, c.ps[:, b, :], ws[sl][:, k, g, :], xn[:, k, t0 + t * 512:t0 + (t + 1) * 512],
                           k == 0, k == 15, reads=[wsb[sl]], writes=[c.psb[b]], signal=(k == 15))
                    cp(p, "scalar", U[g][:, 1 + t * 512:1 + (t + 1) * 512], c.ps[:, b, :], reads=[c.psb[b]],
                       writes=[Ub[g]])
                    act(p, C[g][:, t * 512:(t + 1) * 512], c.ps[:, b, :], AF.Identity, reads=[c.psb[b]],
                        writes=[Cb[g]], scale=cw(g, 1), bias=cb(g))
                stt(p, C[g][:, :], U[g][:, 0:HT], cw(g, 0), C[g][:, :], ALU.mult, ALU.add,
                    reads=[Ub[g], Cb[g]], writes=[Cb[g]])
                stt(p, C[g][:, :], U[g][:, 2:HT + 2], cw(g, 2), C[g][:, :], ALU.mult, ALU.add,
                    reads=[Ub[g], Cb[g]], writes=[Cb[g]])
            act(p, C[0][:, :], C[0][:, :], AF.Silu, reads=[Cb[0]], writes=[Cb[0]])
            asl = 0
            na += 1
            tt(p, "gpsimd", A[asl][:, :], C[0][:, :], C[1][:, :], ALU.mult, reads=[Cb[0], Cb[1]], writes=[Ab[asl]])
            p.dma("gpsimd", c.aT[half * 2:half * 2 + 2, :, fi, :].rearrange("b p t -> p b t"),
                  A[asl][:, :].rearrange("p (b t) -> p b t", b=2), reads=[Ab[asl]])
    p.barrier()
    p.sb_off = base


def phase_B1(p, c, j):
    base = p.sb_off
    xn = p.alloc("xn", [128, 16, S], BF16)
    emit_norm(p, c, c.mixg[:, c.layer * 16:(c.layer + 1) * 16], xn, BF16)
    base2 = p.sb_off
    ost = [p.alloc("ost", [128, S], BF16) for _ in range(2)]
    ostb = [Buf() for _ in range(2)]
    wq = c.W("b_w_qkv")[j]
    c.nproj_banks = 8
    cnt = [0]

    def evac(si, m, t, b):
        g = si // 4
        which = (si % 4) // 2
        h = (si % 2) * 4 + m
        dil = B_GROUPS[g][1]
        L = S // dil
        nj = 512 // dil
        j0 = t * nj
        osl = (si * 4 + m) % 2
        o = ost[osl][:, :].rearrange("p (c j) -> p c j", c=dil)[:, :, j0:j0 + nj]
        i = c.ps[:, b, :].rearrange("p (j c) -> p c j", c=dil)
        cnt[0] += 1
        cp(p, "vector" if (cnt[0] % 2) else "scalar", o, i, reads=[c.psb[b]], writes=[ostb[osl]], merge=True)
        if t == 7:
            row = ((g * 2 + which) * 8 + h) * 128
            p.dma("gpsimd", c.qkb[row:row + 128, :], ost[osl][:, :], reads=[ostb[osl]])

    def wfn(si):
        g = si // 4
        which = (si % 4) // 2
        c0 = g * 3072 + which * 1024 + (si % 2) * 512
        return wq[:, c0:c0 + 512]

    proj_fm(p, c, xn, wfn, 12, 512, evac)
    p.barrier()
    p.sb_off = base2

    def wfv(si):
        g = si // 2
        c0 = g * 3072 + 2048 + (si % 2) * 512
        return wq[:, c0:c0 + 512]

    def store(si, t4, st, stb):
        dst = c.vb[t4 * 512:(t4 + 1) * 512, si * 512:(si + 1) * 512].rearrange("(q p) n -> p q n", p=128)
        p.dma("gpsimd", dst, st[:, :, :], reads=[stb])

    proj_tm(p, c, xn, wfv, 6, store)
    p.barrier()
    p.sb_off = base


def phase_B2(p, c):
    base = p.sb_off
    scale = float(HD) ** -0.5
    Zt = p.alloc("Zt", [128, S], F32)
    Ztb = Buf()
    Ug = [p.alloc("Ug", [128, S], F32) for _ in range(3)]
    Ugb = [Buf() for _ in range(3)]
    bt = [p.alloc("bt", [128, 3, 256], F32) for _ in range(2)]
    btb = [Buf() for _ in range(2)]
    qs = [p.alloc("qs", [128, S], BF16) for _ in range(2)]
    kp = [p.alloc("kp", [128, S + 16 * 128], BF16) for _ in range(2)]
    vp = [p.alloc("vp", [128, 48 * 128], BF16) for _ in range(2)]
    inb = [Buf() for _ in range(2)]
    NL = 6
    lg = [p.alloc("lg", [128, 256], F32) for _ in range(NL)]
    lgb = [Buf() for _ in range(NL)]
    pT = [p.alloc("pT", [128, 256], BF16) for _ in range(NL)]
    pTb = [Buf() for _ in range(NL)]
    yst = [p.alloc("yst", [128, S], BF16) for _ in range(2)]
    ystb = [Buf() for _ in range(2)]
    nin = 0
    nsb = 0
    nl = 0
    nob = 0
    ny = 0
    def load_in(idx):
        hh, g = divmod(idx, 3)
        dil = B_GROUPS[g][1]
        L = S // dil
        nblk = L // 128
        nch = nblk + 1
        sl = idx % 2
        ib = inb[sl]
        MG = dict(merge=True)
        p.dma("sync", bt[sl][:, :, :], c.biasT[hh, g], writes=[btb[sl]])
        rq = ((g * 2 + 0) * 8 + hh) * 128
        rk = ((g * 2 + 1) * 8 + hh) * 128
        p.dma("sync", qs[sl][:, :], c.qkb[rq:rq + 128, :], writes=[ib], **MG)
        kpv = kp[sl][:, 0:dil * (L + 128)].rearrange("p (c j) -> p c j", c=dil)
        p.op("gpsimd", lambda e, o=kpv[:, :, 0:64]: e.memset(o, 0.0), writes=[ib], **MG)
        p.op("gpsimd", lambda e, o=kpv[:, :, 64 + L:128 + L]: e.memset(o, 0.0), writes=[ib], **MG)
        p.dma("sync", kpv[:, :, 64:64 + L], c.qkb[rk:rk + 128, :].rearrange("p (c j) -> p c j", c=dil),
              writes=[ib], **MG)
        vpv = vp[sl][:, 0:dil * nch * 128].rearrange("p (c i d) -> p c i d", c=dil, i=nch)
        col = (g * 8 + hh) * 128
        vv = c.vb[:, col:col + 128].rearrange("(j c) d -> j c d", c=dil)
        p.op("gpsimd", lambda e, o=vpv[0:64, :, 0, :]: e.memset(o, 0.0), writes=[ib], **MG)
        p.op("gpsimd", lambda e, o=vpv[64:128, :, nch - 1, :]: e.memset(o, 0.0), writes=[ib], **MG)
        p.dma("sync", vpv[64:128, :, 0, :], vv[0:64, :, :], writes=[ib], **MG)
        p.dma("sync", vpv[0:64, :, nch - 1, :], vv[L - 64:L, :, :], writes=[ib], **MG)
        if nch - 2 == 1:
            p.dma("sync", vpv[:, :, 1, :], vv[64:L - 64, :, :], writes=[ib], **MG)
        else:
            for cc in range(dil):
                src = vv[64:L - 64, cc, :].rearrange("(i p) d -> p i d", p=128)
                p.dma("sync", vpv[:, cc, 1:nch - 1, :], src, writes=[ib], **MG)

    load_in(0)
    for hh in range(8):
        for g in range(3):
            dil = B_GROUPS[g][1]
            L = S // dil
            nblk = L // 128
            nch = nblk + 1
            sl = nin % 2
            nin += 1
            ib = inb[sl]
            if nin < 24:
                load_in(nin)
            kpv = kp[sl][:, 0:dil * (L + 128)].rearrange("p (c j) -> p c j", c=dil)
            vpv = vp[sl][:, 0:dil * nch * 128].rearrange("p (c i d) -> p c i d", c=dil, i=nch)
            Uv = Ug[g][:, :].rearrange("p (j c) -> p c j", c=dil)
            Zv = Zt[:, :].rearrange("p (j c) -> p c j", c=dil)
            nb4 = min(4, nblk)
            items = []
            for cc in range(dil):
                for b4 in range(nblk // nb4):
                    for qi in range(nb4):
                        items.append((cc, b4, qi))
            sbank = {}

            def qk(ii):
                nonlocal nsb
                cc, b4, qi = items[ii]
                b = b4 * nb4 + qi
                bs = nsb % 4
                nsb += 1
                sbank[ii] = bs
                qap = qs[sl][:, cc * L + b * 128:cc * L + (b + 1) * 128]
                mm(p, c.ps[:, bs, 0:128], kpv[:, cc, b * 128:(b + 1) * 128], qap, True, True,
                   reads=[ib], writes=[c.psb[bs]], signal=False)
                mm(p, c.ps[:, bs, 128:256], kpv[:, cc, (b + 1) * 128:(b + 2) * 128], qap, True, True,
                   reads=[ib], writes=[c.psb[bs]], signal=True)

            wslot = {}

            def front(ii):
                nonlocal nl
                cc, b4, qi = items[ii]
                b = b4 * nb4 + qi
                var = 1 if b == 0 else (2 if b == nblk - 1 else 0)
                bs = sbank.pop(ii)
                w = nl % NL
                nl += 1
                wslot[ii] = w
                stt(p, lg[w][:, :], c.ps[:, bs, 0:256], scale, bt[sl][:, var, :], ALU.mult, ALU.add,
                    reads=[c.psb[bs], btb[sl]], writes=[lgb[w]])
                act(p, pT[w][:, :], lg[w][:, :], AF.Exp, reads=[lgb[w]], writes=[pTb[w]])

            def back(ii):
                nonlocal nob
                cc, b4, qi = items[ii]
                b = b4 * nb4 + qi
                w = wslot.pop(ii)
                if qi == 0:
                    nob += 1
                bo = 4 + (nob % 2)
                bz = 6 + (nob % 2)
                osl = slice(qi * 128, (qi + 1) * 128)
                mm(p, c.ps[:, bo, osl], vpv[:, cc, b, :], pT[w][:, 0:128], True, False,
                   reads=[pTb[w], ib], writes=[c.psb[bo]], signal=False)
                mm(p, c.ps[:, bo, osl], vpv[:, cc, b + 1, :], pT[w][:, 128:256], False, True,
                   reads=[pTb[w], ib], writes=[c.psb[bo]], signal=True)
                mm(p, c.ps[:, bz, osl], c.ones_b[:, :], pT[w][:, 0:128], True, False,
                   reads=[pTb[w]], writes=[c.psb[bz]], signal=False)
                mm(p, c.ps[:, bz, osl], c.ones_b[:, :], pT[w][:, 128:256], False, True,
                   reads=[pTb[w]], writes=[c.psb[bz]], signal=True)
                if qi == nb4 - 1:
                    j0 = b4 * nb4 * 128
                    nj = nb4 * 128
                    cp(p, "scalar", Uv[:, cc, j0:j0 + nj], c.ps[:, bo, 0:nj], reads=[c.psb[bo]], writes=[Ugb[g]])
                    if g == 0:
                        cp(p, "vector", Zv[:, cc, j0:j0 + nj], c.ps[:, bz, 0:nj], reads=[c.psb[bz]], writes=[Ztb])
                    else:
                        tt(p, "vector", Zv[:, cc, j0:j0 + nj], c.ps[:, bz, 0:nj], Zv[:, cc, j0:j0 + nj], ALU.add,
                           reads=[c.psb[bz], Ztb], writes=[Ztb])

            LA = 4
            LF = 2
            nit = len(items)
            for ii in range(min(LA, nit)):
                qk(ii)
            for ii in range(min(LF, nit)):
                front(ii)
            for ii in range(nit):
                if ii + LA < nit:
                    qk(ii + LA)
                if ii + LF < nit:
                    front(ii + LF)
                back(ii)
        p.op("vector", lambda e, o=Zt[:, :]: e.reciprocal(out=o, in_=o), reads=[Ztb], writes=[Ztb])
        for g in range(3):
            ysl = ny % 2
            ny += 1
            tt(p, "vector" if g != 1 else "gpsimd", yst[ysl][:, :], Ug[g][:, :], Zt[:, :], ALU.mult,
               reads=[Ugb[g], Ztb], writes=[ystb[ysl]])
            row = (g * 8 + hh) * 128
            p.dma("gpsimd", c.yT[row:row + 128, :], yst[ysl][:, :], reads=[ystb[ysl]])
    p.barrier()
    p.sb_off = base


def phase_final(p, c):
    base = p.sb_off
    hs = [p.alloc("hs", [128, 16, 512], F32) for _ in range(2)]
    hsb = [Buf() for _ in range(2)]
    sq = [p.alloc("sq", [128, 16, 512], BF16) for _ in range(2)]
    sqb = [Buf() for _ in range(2)]
    rs = [p.alloc("rs", [128, 512], F32) for _ in range(2)]
    rsb = [Buf() for _ in range(2)]
    xs = p.alloc("xs", [128, 16, 512], F32)
    xsb = Buf()
    ot = [p.alloc("ot", [128, D], F32) for _ in range(2)]
    otb = [Buf() for _ in range(2)]
    hTv = c.hT.rearrange("(c p) t -> p c t", p=128)
    no = 0
    for j in range(8):
        sl = j % 2
        p.dma("sync", hs[sl][:, :, :], hTv[:, :, j * 512:(j + 1) * 512], writes=[hsb[sl]])
        b = c.nbk % 8
        c.nbk += 1
        act(p, sq[sl][:, :, :], hs[sl][:, :, :], AF.Square, reads=[hsb[sl]], writes=[sqb[sl]])
        for k in range(16):
            mm(p, c.ps[:, b, :], c.ones_db[:, :], sq[sl][:, k, :], k == 0, k == 15,
               reads=[sqb[sl]], writes=[c.psb[b]], signal=(k == 15))
        act(p, rs[sl][:, :], c.ps[:, b, :], AF.Sqrt, reads=[c.psb[b]], writes=[rsb[sl]], bias=c.eps_t[:, 0:1])
        p.op("vector", lambda e, o=rs[sl][:, :]: e.reciprocal(out=o, in_=o), reads=[rsb[sl]], writes=[rsb[sl]])
        for k in range(16):
            stt(p, xs[:, k, :], hs[sl][:, k, :], c.fing[:, k:k + 1], rs[sl][:, :],
                ALU.mult, ALU.mult, reads=[hsb[sl], rsb[sl]], writes=[xsb])
        for sub in range(4):
            osl = no % 2
            no += 1
            for c4 in range(4):
                b = c.nbk % 8
                c.nbk += 1
                for jj in range(4):
                    k = c4 * 4 + jj
                    o = c.ps[:, b, jj * 128:(jj + 1) * 128]
                    i = xs[:, k, sub * 128:(sub + 1) * 128]
                    p.op("tensor", lambda e, o=o, i=i: e.transpose(o, i, c.ident_f[:, :]),
                         reads=[xsb], writes=[c.psb[b]], signal=(jj == 3))
                cp(p, "vector" if (c4 % 2) else "scalar", ot[osl][:, c4 * 512:(c4 + 1) * 512], c.ps[:, b, :],
                   reads=[c.psb[b]], writes=[otb[osl]], merge=True)
            t = j * 4 + sub
            p.dma("gpsimd", c.out[t * 128:(t + 1) * 128, :], ot[osl][:, :], reads=[otb[osl]])
    p.barrier()
    p.sb_off = base


PHASES_ALL = None


def build(stop_after=None, dbg=()):
    nc = bass.Bass("TRN2", target_bir_lowering=False)
    es = ExitStack()
    c = Ctx()

    def din(name, shape, dt=F32):
        return nc.dram_tensor(name, list(shape), dt, kind="ExternalInput").ap()

    def dscr(name, shape, dt):
        return nc.dram_tensor(name, list(shape), dt, kind=("ExternalOutput" if (dbg and name in dbg) else "Internal")).ap()

    c.x = din("x", [S, D])
    wshapes = {"a_w_qkv": [2, D, 3072], "a_w_o": [2, D, D], "b_w_qkv": [2, D, 9216], "b_w_o": [2, 3072, D],
               "w_up": [4, D, 2 * DFF], "w_down": [4, DFF, D]}
    wcache = {}

    def W(name):
        if name not in wcache:
            wcache[name] = din(name, wshapes[name])
        return wcache[name]

    c.W = W
    import os
    if os.environ.get("MK_ALLW"):
        for k_ in wshapes:
            W(k_)
    smalls = {
        "mixg": 64, "ffng": 64, "fing": 16, "aqg": 2, "akg": 2, "convw": 4 * 2 * 3 * NFT, "convb": 4 * 2 * NFT,
        "ident_f": 128, "ones_f": 128, "ones_d": 128, "ones_h": 128, "perm": 128, "eps_t": 1,
    }
    sd = {k: din(k + "_d", [128, n]) for k, n in smalls.items()}
    ones_b_d = din("ones_b_d", [128, 128], BF16)
    c.cos_d = din("cos_d", [128, S])
    c.sin_d = din("sin_d", [128, S])
    c.biasT = din("biasT", [8, 3, 128, 3, 256])
    c.out = nc.dram_tensor("out", [S, D], F32, kind="ExternalOutput").ap()
    c.hT = dscr("hT", [D, S], F32)
    c.qT = dscr("qT", [2048, S], BF16)
    c.kT = dscr("kT", [512, S], BF16)
    c.v = dscr("v", [S, 512], BF16)
    c.aoT = dscr("aoT", [2048, S], BF16)
    c.aT = dscr("aT", [4, 128, NFT, 1024], BF16)
    c.qkb = dscr("qkb", [3 * 2 * 8 * 128, S], BF16)
    c.vb = dscr("vb", [S, 3072], BF16)
    c.yT = dscr("yT", [3072, S], BF16)
    c.dbgz = dscr("dbgz", [128, 1028], F32) if (dbg and "dbgz" in dbg) else None

    p = Prog(nc, es)
    c.ps = nc.alloc_psum_tensor("ps", [128, 8, 512], F32)
    c.psb = [Buf() for _ in range(8)]
    c.nbk = 0
    c.nproj_banks = 8
    cb = Buf()
    for k, n in smalls.items():
        t = p.alloc(k, [128, n], F32)
        setattr(c, k, t)
        p.dma("sync", t[:, :], sd[k], writes=[cb])
    c.ones_b = p.alloc("ones_b", [128, 128], BF16)
    p.dma("sync", c.ones_b[:, :], ones_b_d, writes=[cb])
    c.ones_db = p.alloc("ones_db", [128, 128], BF16)
    c.ones_hb = p.alloc("ones_hb", [128, 128], BF16)
    cb2 = Buf()
    p.op("vector", lambda e: e.tensor_copy(out=c.ones_db[:, :], in_=c.ones_d[:, :]), reads=[cb], writes=[cb2])
    p.op("vector", lambda e: e.tensor_copy(out=c.ones_hb[:, :], in_=c.ones_h[:, :]), reads=[cb], writes=[cb2])
    p.barrier()

    def run():
        phase_x_to_hT(p, c)
        if stop_after == "T0":
            return
        for l in range(DEPTH):
            c.layer = l
            j = l // 2
            if l % 2 == 0:
                phase_A1(p, c, j)
                if stop_after == f"A1_{l}":
                    return
                phase_A2(p, c)
                if stop_after == f"A2_{l}":
                    return
                phase_proj_res(p, c, c.aoT, c.W("a_w_o")[j], 16, 2048)
            else:
                phase_B1(p, c, j)
                if stop_after == f"B1_{l}":
                    return
                phase_B2(p, c)
                if stop_after == f"B2_{l}":
                    return
                phase_proj_res(p, c, c.yT, c.W("b_w_o")[j], 24, 2048)
            if stop_after == f"M_{l}":
                return
            phase_F1(p, c, l)
            if stop_after == f"F1_{l}":
                return
            phase_proj_res(p, c, c.aT, c.W("w_down")[l], NFT, 1024, SW=256, tiled=True)
            if stop_after == f"F_{l}":
                return
        phase_final(p, c)

    run()
    with nc.Block() as block:
        p.flush(block)
    es.close()
    p.used_inputs = set(wcache)
    return nc, p


def _t5_bucket(rel):
    nb = 16
    max_exact = 8
    base = np.where(rel > 0, nb, 0)
    n = np.abs(rel)
    nf = np.maximum(n, 1).astype(np.float32)
    large = max_exact + (np.log(nf / np.float32(max_exact)) / np.float32(np.log(1024 / max_exact))
                         * (nb - max_exact)).astype(np.int32)
    large = np.minimum(large, nb - 1)
    return base + np.where(n < max_exact, n, large)


def _consts():
    half, quarter = 64, 32
    inv_freq = (np.float32(10000.0) ** (-np.arange(quarter, dtype=np.float32) / np.float32(quarter))).astype(np.float32)
    t = np.arange(S)
    row = (t // 64).astype(np.float32)
    colp = (t % 64).astype(np.float32)
    cosT = np.zeros((128, S), np.float32)
    sinT = np.zeros((128, S), np.float32)
    perm = np.zeros((128, 128), np.float32)
    for d in range(128):
        pos = row if d < 64 else colp
        i = d % 64
        fi = i % 32
        ang = (pos * inv_freq[fi]).astype(np.float32)
        cosT[d] = np.cos(ang)
        first = i < 32
        sinT[d] = (-np.sin(ang) if first else np.sin(ang))
        pd = d + 32 if first else d - 32
        perm[pd, d] = 1.0
    return cosT, sinT, perm


def _chunk_cols(v, n):
    return v


def kernel(x, a_w_qkv, a_w_o, a_q_gain, a_k_gain, b_w_qkv, b_w_o, rel_bias,
           mix_norm, ffn_norm, w_up, conv_w, conv_b, w_down, final_norm, _stop_after=None, _dbg=(), _ncores=8):
    f32 = np.float32
    x = np.asarray(x, f32)
    nb = x.shape[0]
    cosT, sinT, perm = _consts()

    def pc(v, n):
        v = np.asarray(v, f32)
        lead = v.shape[:-1]
        r = v.reshape(lead + (n, 128))
        r = np.moveaxis(r, -1, 0)
        return np.ascontiguousarray(r.reshape(128, -1))

    smalls = {
        "mixg_d": pc(mix_norm, 16), "ffng_d": pc(ffn_norm, 16), "fing_d": pc(final_norm, 16),
        "aqg_d": np.ascontiguousarray(np.asarray(a_q_gain, f32).T), "akg_d": np.ascontiguousarray(np.asarray(a_k_gain, f32).T),
        "convw_d": np.ascontiguousarray(np.asarray(conv_w, f32).reshape(4, 3, 2, NFT, 128).transpose(4, 0, 2, 1, 3).reshape(128, -1)),
        "convb_d": np.ascontiguousarray(np.asarray(conv_b, f32).reshape(4, 2, NFT, 128).transpose(3, 0, 1, 2).reshape(128, -1)),
        "ident_f_d": np.eye(128, dtype=f32), "ones_f_d": np.ones((128, 128), f32), "ones_d_d": np.full((128, 128), 1.0 / D, f32),
        "ones_h_d": np.full((128, 128), 1.0 / HD, f32), "perm_d": perm, "eps_t_d": np.full((128, 1), EPS, f32),
        "ones_b_d": np.ones((128, 128), ml_dtypes.bfloat16),
        "cos_d": cosT, "sin_d": sinT,
    }
    rb = np.asarray(rel_bias, f32)
    pp = np.arange(128)[:, None]
    ff = np.arange(128)[None, :]
    biasT = np.empty((8, 3, 128, 3, 256), f32)
    for g, (_, dil) in enumerate(B_GROUPS):
        relA = pp - 64 - ff
        relB = pp + 64 - ff
        okA = np.abs(relA) <= 64
        okB = np.abs(relB) <= 64
        bkA = _t5_bucket(relA * dil)
        bkB = _t5_bucket(relB * dil)
        for hh in range(8):
            col = rb[:, g * 8 + hh]
            A = np.where(okA, col[bkA], f32(NEG))
            Bm = np.where(okB, col[bkB], f32(NEG))
            Ae = np.where(okA & (pp >= 64), col[bkA], f32(NEG))
            Be = np.where(okB & (pp < 64), col[bkB], f32(NEG))
            biasT[hh, g, :, 0, :128] = A
            biasT[hh, g, :, 0, 128:] = Bm
            biasT[hh, g, :, 1, :128] = Ae
            biasT[hh, g, :, 1, 128:] = Bm
            biasT[hh, g, :, 2, :128] = A
            biasT[hh, g, :, 2, 128:] = Be
    shared = dict(smalls)
    shared.update({
        "a_w_qkv": np.asarray(a_w_qkv, f32), "a_w_o": np.asarray(a_w_o, f32), "b_w_qkv": np.asarray(b_w_qkv, f32),
        "b_w_o": np.asarray(b_w_o, f32), "w_up": np.asarray(w_up, f32), "w_down": np.asarray(w_down, f32),
        "biasT": biasT,
    })
    nc, p = build(stop_after=_stop_after, dbg=_dbg)
    ncores = _ncores
    in_maps = []
    for k in list(shared):
        if k in ("a_w_qkv", "a_w_o", "b_w_qkv", "b_w_o", "w_up", "w_down") and k not in p.used_inputs:
            del shared[k]
    for i in range(ncores):
        m = dict(shared)
        m["x"] = np.ascontiguousarray(x[i % nb])
        in_maps.append(m)
    res = run_bass_kernel_spmd(nc, in_maps, core_ids=list(range(ncores)))
    if _dbg:
        return res
    return np.stack([res.results[i]["out"] for i in range(ncores)], axis=0).astype(f32)
```

```python
import numpy as np
import ml_dtypes
from contextlib import ExitStack
import concourse.bass as bass
import concourse.mybir as mybir
from concourse.bass_utils import run_bass_kernel_spmd

F32 = mybir.dt.float32
BF16 = mybir.dt.bfloat16
AF = mybir.ActivationFunctionType
ALU = mybir.AluOpType

S = 4096
D = 2048
DFF = 5632
NFT = 44
HD = 128
DEPTH = 4
EPS = 1e-6
NEG = -30000.0
B_GROUPS = ((128, 1), (512, 4), (2048, 16))
SB_BASE = 16512
SB_END = 229376
DT_SIZE = {F32: 4, BF16: 2}


class Sem:
    def __init__(self, h):
        self.h = h
        self.n = 0


class Buf:
    __slots__ = ("w", "r")

    def __init__(self):
        self.w = {}
        self.r = {}


class Prog:
    ENGS = ("sync", "scalar", "vector", "gpsimd", "tensor")
    DMAE = ("sync", "gpsimd", "scalar")
    NDS = 12

    def __init__(self, nc, es):
        self.nc, self.es = nc, es
        self.q = {e: [] for e in self.ENGS}
        self.esem = {e: self.newsem("s_" + e) for e in self.ENGS}
        self.dsem = {e: [self.newsem(f"d_{e}{i}") for i in range(self.NDS)] for e in self.DMAE}
        self.dctr = {e: 0 for e in self.DMAE}
        self.bar = self.newsem("bar")
        self.nbar = 0
        self.known = {e: {} for e in self.ENGS}
        self.sb_off = SB_BASE
        self.uid = 0
        self.nops = 0

    def newsem(self, name):
        return Sem(self.es.enter_context(self.nc.semaphore(name)))

    def alloc(self, name, shape, dtype):
        n = DT_SIZE[dtype]
        for s in shape[1:]:
            n *= s
        off = (self.sb_off + 63) // 64 * 64
        assert off + n <= SB_END, (name, off, n)
        self.sb_off = off + n
        self.uid += 1
        return self.nc.alloc_sbuf_tensor_at(f"{name}_{self.uid}", list(shape), dtype, offset=off)

    def _waits(self, eng, reads, writes, is_dma, merge=False):
        need = {}
        for b in reads:
            for s, v in b.w.items():
                if need.get(s, 0) < v:
                    need[s] = v
        for b in writes:
            if not merge:
                for s, v in b.w.items():
                    if need.get(s, 0) < v:
                        need[s] = v
            for s, v in b.r.items():
                if need.get(s, 0) < v:
                    need[s] = v
        kn = self.known[eng]
        own = self.esem[eng]
        out = []
        for s, v in need.items():
            if s is own and not is_dma:
                continue
            if kn.get(s, 0) >= v:
                continue
            kn[s] = v
            out.append((s, v))
        return out

    def _book(self, s, val, reads, writes, merge=False):
        for b in reads:
            b.r[s] = val
        for b in writes:
            if merge:
                b.w[s] = val
            else:
                b.w = {s: val}
                b.r = {}

    def op(self, eng, fn, reads=(), writes=(), signal=True, merge=False, own=False):
        waits = self._waits(eng, reads, writes, own, merge)
        self.nops += 1
        if signal:
            s = self.esem[eng]
            s.n += 1
            self._book(s, s.n, reads, writes, merge)
            self.q[eng].append((fn, waits, s, 1))
        else:
            self.q[eng].append((fn, waits, None, 0))

    def dma(self, eng, out, in_, reads=(), writes=(), merge=False):
        k = self.dctr[eng] % self.NDS
        self.dctr[eng] += 1
        s = self.dsem[eng][k]
        waits = self._waits(eng, reads, writes, True, merge)
        kn = self.known[eng]
        if s.n > 0 and kn.get(s, 0) < s.n:
            waits.append((s, s.n))
            kn[s] = s.n
        s.n += 16
        self._book(s, s.n, reads, writes, merge)
        self.nops += 1
        self.q[eng].append((lambda e: e.dma_start(out=out, in_=in_), waits, s, 16))

    def barrier(self):
        for e in self.DMAE:
            waits = []
            for s in self.dsem[e]:
                if s.n > self.known[e].get(s, 0):
                    waits.append((s, s.n))
            if waits:
                self.q[e].append((None, waits, None, 0))
        self.nbar += 1
        tgt = 5 * self.nbar
        for e in self.ENGS:
            self.q[e].append((lambda en: en.drain(), [], self.bar, 1))
            self.q[e].append((None, [(self.bar, tgt)], None, 0))
        allsems = list(self.esem.values()) + [s for e in self.DMAE for s in self.dsem[e]]
        for e in self.ENGS:
            for s in allsems:
                self.known[e][s] = s.n

    def flush(self, block):
        for name in self.ENGS:
            ops = self.q[name]

            def body(e, ops=ops):
                for fn, waits, s, amt in ops:
                    for ws, wv in waits:
                        e.wait_ge(ws.h, wv)
                    if fn is None:
                        continue
                    ins = fn(e)
                    if s is not None:
                        ins.then_inc(s.h, amt)

            getattr(block, name)(body)


def mm(p, out, lhsT, rhs, start, stop, reads=(), writes=(), signal=False):
    p.op("tensor", lambda e: e.matmul(out, lhsT=lhsT, rhs=rhs, start=start, stop=stop), reads, writes, signal)


def act(p, out, in_, func, reads=(), writes=(), scale=None, bias=None, eng="scalar"):
    kw = {}
    if scale is not None:
        kw["scale"] = scale
    if bias is not None:
        kw["bias"] = bias
    p.op("scalar", lambda e: e.activation(out=out, in_=in_, func=func, **kw), reads, writes)


def tt(p, eng, out, in0, in1, op, reads=(), writes=()):
    p.op(eng, lambda e: e.tensor_tensor(out=out, in0=in0, in1=in1, op=op), reads, writes)


def stt(p, out, in0, scalar, in1, op0, op1, reads=(), writes=()):
    p.op("vector", lambda e: e.scalar_tensor_tensor(out=out, in0=in0, scalar=scalar, in1=in1, op0=op0, op1=op1),
         reads, writes)


def cp(p, eng, out, in_, reads=(), writes=(), merge=False):
    if eng == "scalar":
        p.op("scalar", lambda e: e.activation(out=out, in_=in_, func=AF.Copy), reads, writes, merge=merge)
    else:
        p.op(eng, lambda e: e.tensor_copy(out=out, in_=in_), reads, writes, merge=merge)


class Ctx:
    pass


class WLoader:
    PIECE = 1024

    def __init__(self, p, nstage=4):
        self.p = p
        self.st = [p.alloc("wst", [128, self.PIECE], F32) for _ in range(nstage)]
        self.stb = [Buf() for _ in range(nstage)]
        self.n = 0

    def load(self, dst, src, K, N, dstbuf):
        p = self.p
        kk = max(1, self.PIECE // N)
        for k0 in range(0, K, kk):
            k1 = min(K, k0 + kk)
            i = self.n % len(self.st)
            eng = "gpsimd" if (self.n % 2 == 0) else "vector"
            self.n += 1
            stv = self.st[i][:, 0:(k1 - k0) * N].rearrange("p (k n) -> p k n", n=N)
            p.dma("sync", stv, src[:, k0:k1, :], writes=[self.stb[i]])
            p.op(eng, lambda e, o=dst[:, k0:k1, :], i_=stv: e.tensor_copy(out=o, in_=i_),
                 reads=[self.stb[i]], writes=[dstbuf], merge=True)


def phase_x_to_hT(p, c):
    base = p.sb_off
    xs = [p.alloc("xs", [128, D], F32) for _ in range(2)]
    xsb = [Buf() for _ in range(2)]
    stg = [p.alloc("stg", [128, 16, 512], F32) for _ in range(2)]
    stgb = [Buf() for _ in range(2)]
    hTv = c.hT.rearrange("(c p) t -> p c t", p=128)
    bk = 0
    for g in range(8):
        sl = g % 2
        for sub in range(4):
            t = g * 4 + sub
            xsl = t % 2
            p.dma("sync", xs[xsl][:, :], c.x[t * 128:(t + 1) * 128, :], writes=[xsb[xsl]])
            for c4 in range(4):
                b = bk % 8
                bk += 1
                for j in range(4):
                    cc = c4 * 4 + j
                    o = c.ps[:, b, j * 128:(j + 1) * 128]
                    i = xs[xsl][:, cc * 128:(cc + 1) * 128]
                    p.op("tensor", lambda e, o=o, i=i: e.transpose(o, i, c.ident_f[:, :]),
                         reads=[xsb[xsl]], writes=[c.psb[b]], signal=(j == 3))
                o = stg[sl][:, c4 * 4:(c4 + 1) * 4, sub * 128:(sub + 1) * 128]
                i = c.ps[:, b, :].rearrange("p (j t) -> p j t", j=4)
                cp(p, "vector" if (c4 % 2) else "scalar", o, i, reads=[c.psb[b]], writes=[stgb[sl]], merge=True)
        p.dma("gpsimd", hTv[:, :, g * 512:(g + 1) * 512], stg[sl][:, :, :], reads=[stgb[sl]])
    p.barrier()
    p.sb_off = base


def emit_norm(p, c, gain, xn, dt_out, nt0=0, nt1=8):
    base = p.sb_off
    TW = 256
    hs = [p.alloc("hs", [128, 16, TW], F32) for _ in range(3)]
    hsb = [Buf() for _ in range(3)]
    sq = [p.alloc("sq", [128, 16, TW], BF16) for _ in range(2)]
    sqb = [Buf() for _ in range(2)]
    rs = [p.alloc("rs", [128, TW], F32) for _ in range(2)]
    rsb = [Buf() for _ in range(2)]
    xnb = Buf()
    hTv = c.hT.rearrange("(c p) t -> p c t", p=128)
    r = 512 // TW
    for j in range(nt0 * r, nt1 * r):
        sl = j % 2
        hl = j % 3
        p.dma("sync", hs[hl][:, :, :], hTv[:, :, j * TW:(j + 1) * TW], writes=[hsb[hl]])
        b = c.nbk % 8
        c.nbk += 1
        act(p, sq[sl][:, :, :], hs[hl][:, :, :], AF.Square, reads=[hsb[hl]], writes=[sqb[sl]])
        for k in range(16):
            mm(p, c.ps[:, b, 0:TW], c.ones_db[:, :], sq[sl][:, k, :], k == 0, k == 15,
               reads=[sqb[sl]], writes=[c.psb[b]], signal=(k == 15))
        act(p, rs[sl][:, :], c.ps[:, b, 0:TW], AF.Sqrt, reads=[c.psb[b]], writes=[rsb[sl]], bias=c.eps_t[:, 0:1])
        p.op("vector", lambda e, o=rs[sl][:, :]: e.reciprocal(out=o, in_=o), reads=[rsb[sl]], writes=[rsb[sl]])
        jj = j - nt0 * r
        for k in range(16):
            stt(p, xn[:, k, jj * TW:(jj + 1) * TW], hs[hl][:, k, :], gain[:, k:k + 1], rs[sl][:, :],
                ALU.mult, ALU.mult, reads=[hsb[hl], rsb[sl]], writes=[xnb])
    p.barrier()
    p.sb_off = base


def proj_fm(p, c, xn, wfn, nslab, slab_w, evac, nk=16, ntt=8):
    ws = [p.alloc("ws", [128, nk, slab_w], BF16) for _ in range(2)]
    wsb = [Buf() for _ in range(2)]
    wl = WLoader(p)

    def lw(si):
        wl.load(ws[si % 2][:, :, :], wfn(si).rearrange("(k p) n -> p k n", p=128), nk, slab_w, wsb[si % 2])

    lw(0)
    for si in range(nslab):
        sl = si % 2
        if si + 1 < nslab:
            lw(si + 1)
        for m in range(slab_w // 128):
            for t in range(ntt):
                b = c.nbk % c.nproj_banks
                c.nbk += 1
                for k in range(nk):
                    mm(p, c.ps[:, b, :], ws[sl][:, k, m * 128:(m + 1) * 128], xn[:, k, t * 512:(t + 1) * 512],
                       k == 0, k == nk - 1, reads=[wsb[sl]], writes=[c.psb[b]], signal=(k == nk - 1))
                evac(si, m, t, b)


def proj_tm(p, c, xn, wfn, nslab, store, nk=16):
    ws = [p.alloc("wv", [128, nk, 512], BF16) for _ in range(2)]
    wsb = [Buf() for _ in range(2)]
    st = [p.alloc("vst", [128, 4, 512], BF16) for _ in range(2)]
    stb = [Buf() for _ in range(2)]
    n = 0
    wl = WLoader(p)

    def lw(si):
        wl.load(ws[si % 2][:, :, :], wfn(si).rearrange("(k p) n -> p k n", p=128), nk, 512, wsb[si % 2])

    lw(0)
    for si in range(nslab):
        sl = si % 2
        if si + 1 < nslab:
            lw(si + 1)
        for t4 in range(8):
            ssl = n % 2
            n += 1
            for tq in range(4):
                t = t4 * 4 + tq
                b = c.nbk % c.nproj_banks
                c.nbk += 1
                for k in range(nk):
                    mm(p, c.ps[:, b, :], xn[:, k, t * 128:(t + 1) * 128], ws[sl][:, k, :],
                       k == 0, k == nk - 1, reads=[wsb[sl]], writes=[c.psb[b]], signal=(k == nk - 1))
                cp(p, "vector" if (tq % 2) else "scalar", st[ssl][:, tq, :], c.ps[:, b, :],
                   reads=[c.psb[b]], writes=[stb[ssl]], merge=True)
            store(si, t4, st[ssl], stb[ssl])


def phase_A1(p, c, j):
    base = p.sb_off
    xn = p.alloc("xn", [128, 16, S], BF16)
    emit_norm(p, c, c.mixg[:, c.layer * 16:(c.layer + 1) * 16], xn, BF16)
    base2 = p.sb_off
    cst = [p.alloc("cst", [128, 2, 512], F32) for _ in range(2)]
    cstb = [Buf() for _ in range(2)]
    NW = 2
    qg = [p.alloc("qg", [128, 512], F32) for _ in range(NW)]
    sq = [p.alloc("sq", [128, 512], BF16) for _ in range(NW)]
    rs = [p.alloc("rs", [128, 512], F32) for _ in range(NW)]
    t1 = [p.alloc("t1", [128, 512], F32) for _ in range(NW)]
    t2 = [p.alloc("t2", [128, 512], F32) for _ in range(NW)]
    qgb = [Buf() for _ in range(NW)]
    sqb = [Buf() for _ in range(NW)]
    rsb = [Buf() for _ in range(NW)]
    t1b = [Buf() for _ in range(NW)]
    t2b = [Buf() for _ in range(NW)]
    ost = [p.alloc("ost", [128, 512], BF16) for _ in range(3)]
    ostb = [Buf() for _ in range(3)]
    wq = c.W("a_w_qkv")[j]
    cnt = [0]
    c.nproj_banks = 4

    def evac(si, m, t, b):
        head = si * 2 + m
        isq = head < 16
        gcol = c.aqg[:, j:j + 1] if isq else c.akg[:, j:j + 1]
        w = cnt[0] % NW
        cs = cnt[0] % 2
        cnt[0] += 1
        osl = head % 2
        p.dma("sync", cst[cs][:, 0, :], c.cos_d[:, t * 512:(t + 1) * 512], writes=[cstb[cs]], merge=True)
        p.dma("sync", cst[cs][:, 1, :], c.sin_d[:, t * 512:(t + 1) * 512], writes=[cstb[cs]], merge=True)
        bs = 4 + (cnt[0] % 2) * 2
        act(p, qg[w][:, :], c.ps[:, b, :], AF.Identity, reads=[c.psb[b]], writes=[qgb[w]], scale=gcol)
        act(p, sq[w][:, :], c.ps[:, b, :], AF.Square, reads=[c.psb[b]], writes=[sqb[w]])
        mm(p, c.ps[:, bs, :], c.ones_hb[:, :], sq[w][:, :], True, True, reads=[sqb[w]], writes=[c.psb[bs]], signal=True)
        mm(p, c.ps[:, bs + 1, :], c.perm[:, :], qg[w][:, :], True, True, reads=[qgb[w]], writes=[c.psb[bs + 1]],
           signal=True)
        act(p, rs[w][:, :], c.ps[:, bs, :], AF.Sqrt, reads=[c.psb[bs]], writes=[rsb[w]], bias=c.eps_t[:, 0:1])
        p.op("vector", lambda e, o=rs[w][:, :]: e.reciprocal(out=o, in_=o), reads=[rsb[w]], writes=[rsb[w]])
        tsl = slice(t * 512, (t + 1) * 512)
        tt(p, "vector", t1[w][:, :], qg[w][:, :], cst[cs][:, 0, :], ALU.mult, reads=[qgb[w], cstb[cs]],
           writes=[t1b[w]])
        tt(p, "vector", t2[w][:, :], c.ps[:, bs + 1, :], cst[cs][:, 1, :], ALU.mult, reads=[c.psb[bs + 1], cstb[cs]],
           writes=[t2b[w]])
        tt(p, "gpsimd", t1[w][:, :], t1[w][:, :], t2[w][:, :], ALU.add, reads=[t2b[w], t1b[w]], writes=[t1b[w]])
        osl = cnt[0] % 3
        tt(p, "vector", ost[osl][:, :], t1[w][:, :], rs[w][:, :], ALU.mult, reads=[t1b[w], rsb[w]],
           writes=[ostb[osl]])
        dst = c.qT[head * 128:(head + 1) * 128, tsl] if isq else c.kT[(head - 16) * 128:(head - 15) * 128, tsl]
        p.dma("gpsimd", dst, ost[osl][:, :], reads=[ostb[osl]])

    proj_fm(p, c, xn, lambda si: wq[:, si * 256:(si + 1) * 256], 10, 256, evac)
    p.barrier()
    p.sb_off = base2
    c.nproj_banks = 8

    def store(si, t4, st, stb):
        dst = c.v[t4 * 512:(t4 + 1) * 512, :].rearrange("(q p) n -> p q n", p=128)
        p.dma("gpsimd", dst, st[:, :, :], reads=[stb])

    proj_tm(p, c, xn, lambda si: wq[:, 2560:3072], 1, store)
    p.barrier()
    p.sb_off = base


def phase_A2(p, c):
    base = p.sb_off
    scale = float(HD) ** -0.5
    kT = [p.alloc("kT", [128, S], BF16) for _ in range(2)]
    vg = [p.alloc("vg", [128, 32, 128], BF16) for _ in range(2)]
    kvb = [Buf() for _ in range(2)]
    qh = [p.alloc("qh", [128, S], BF16) for _ in range(2)]
    qhb = [Buf() for _ in range(2)]
    NP = 4
    pT = [p.alloc("pT", [128, 512], BF16) for _ in range(NP)]
    pTb = [Buf() for _ in range(NP)]
    rzc = [p.alloc("rzc", [128, 4], F32) for _ in range(2)]
    rzcb = [Buf() for _ in range(2)]
    Dm = [p.alloc("Dm", [128, 4, 128], F32) for _ in range(2)]
    Dmb = [Buf() for _ in range(2)]
    zr = [p.alloc("zr", [128, 512], F32) for _ in range(2)]
    zrb = [Buf() for _ in range(2)]
    ao = [p.alloc("ao", [128, S], BF16) for _ in range(2)]
    aob = [Buf() for _ in range(2)]
    ns = 0
    nq = 0
    npt = 0
    deferred = []

    def lkv(g_):
        p.dma("sync", kT[g_ % 2][:, :], c.kT[g_ * 128:(g_ + 1) * 128, :], writes=[kvb[g_ % 2]], merge=True)
        p.dma("sync", vg[g_ % 2][:, :, :], c.v[:, g_ * 128:(g_ + 1) * 128].rearrange("(c p) d -> p c d", p=128),
              writes=[kvb[g_ % 2]], merge=True)

    def lq(h_):
        p.dma("sync", qh[h_ % 2][:, :], c.qT[h_ * 128:(h_ + 1) * 128, :], writes=[qhb[h_ % 2]])

    lkv(0)
    lq(0)
    for g in range(4):
        ksl = g % 2
        for hq in range(4):
            head = g * 4 + hq
            hsl = head % 2
            if head + 1 < 16:
                lq(head + 1)
            if hq == 3 and g + 1 < 4:
                lkv(g + 1)
            for qt in range(8):
                bo = 3 + (nq % 2)
                bz = 5 + (nq % 2)
                zsl = nq % 2
                nq += 1
                qsl = slice(qt * 512, (qt + 1) * 512)

                def qk(kc):
                    nonlocal ns
                    b = ns % 3
                    ns += 1
                    mm(p, c.ps[:, b, :], kT[ksl][:, kc * 128:(kc + 1) * 128], qh[hsl][:, qsl], True, True,
                       reads=[kvb[ksl], qhb[hsl]], writes=[c.psb[b]], signal=True)
                    return b

                def pv(kc, b):
                    nonlocal npt
                    w = npt % NP
                    npt += 1
                    act(p, pT[w][:, :], c.ps[:, b, :], AF.Exp, reads=[c.psb[b]], writes=[pTb[w]], scale=scale)
                    last = kc == 31
                    mm(p, c.ps[:, bo, :], vg[ksl][:, kc, :], pT[w][:, :], kc == 0, last,
                       reads=[pTb[w], kvb[ksl]], writes=[c.psb[bo]], signal=last)
                    for s_ in range(4):
                        mm(p, c.ps[:, bz, kc * 4 + s_:kc * 4 + s_ + 1], pT[w][:, s_ * 128:(s_ + 1) * 128],
                           c.ones_b[:, 0:1], True, True,
                           reads=[pTb[w]], writes=[c.psb[bz]], signal=(s_ == 3))

                b0 = qk(0)
                b1 = qk(1)
                bl = [b0, b1]
                for kc in range(32):
                    if kc + 2 < 32:
                        bl.append(qk(kc + 2))
                    pv(kc, bl[kc])
                    if kc == 4 and deferred:
                        for f_ in deferred:
                            f_()
                        deferred.clear()
                p.op("vector", lambda e, o=rzc[zsl][:, :], i=c.ps[:, bz, 0:128].rearrange("p (k s) -> p s k", s=4):
                     e.tensor_reduce(out=o, in_=i, axis=mybir.AxisListType.X, op=ALU.add),
                     reads=[c.psb[bz]], writes=[rzcb[zsl]])
                p.op("vector", lambda e, o=rzc[zsl][:, :]: e.reciprocal(out=o, in_=o),
                     reads=[rzcb[zsl]], writes=[rzcb[zsl]], own=True)
                for s_ in range(4):
                    p.op("vector", lambda e, o=Dm[zsl][:, s_, :], sc=rzc[zsl][:, s_:s_ + 1]: e.tensor_scalar(
                        out=o, in0=c.ident_f[:, :], scalar1=sc, scalar2=None, op0=ALU.mult),
                        reads=[rzcb[zsl]], writes=[Dmb[zsl]], own=True)

                def fin2(zsl=zsl, bo=bo, hsl=hsl, qsl=qsl):
                    mm(p, c.ps[:, 7, :], c.ones_f[:, :], Dm[zsl][:, :, :].rearrange("p s q -> p (s q)"), True, True,
                       reads=[Dmb[zsl]], writes=[c.psb[7]], signal=True)
                    cp(p, "vector", zr[zsl][:, :], c.ps[:, 7, :], reads=[c.psb[7]], writes=[zrb[zsl]])
                    tt(p, "vector", ao[hsl][:, qsl], c.ps[:, bo, :], zr[zsl][:, :], ALU.mult,
                       reads=[c.psb[bo], zrb[zsl]], writes=[aob[hsl]])

                deferred.append(fin2)
                if head == 0 and qt == 0 and getattr(c, "dbgz", None) is not None:
                    def dbg_(zsl=zsl, bz=bz):
                        p.dma("gpsimd", c.dbgz[:, 0:512], zr[zsl][:, :], reads=[zrb[zsl]])
                        p.dma("gpsimd", c.dbgz[:, 512:516], rzc[zsl][:, :], reads=[rzcb[zsl]])
                        p.dma("gpsimd", c.dbgz[:, 516:1028], Dm[zsl][:, :, :].rearrange("p s q -> p (s q)"), reads=[Dmb[zsl]])
                    deferred.append(dbg_)
                if qt == 7:
                    def st(head=head, hsl=hsl):
                        p.dma("gpsimd", c.aoT[head * 128:(head + 1) * 128, :], ao[hsl][:, :], reads=[aob[hsl]])
                    deferred.append(st)
    for f_ in deferred:
        f_()
    p.barrier()
    p.sb_off = base


def phase_proj_res(p, c, actT, W, nk, TB, SW=512, tiled=False):
    base = p.sb_off
    at = p.alloc("at", [128, nk, TB], BF16)
    ws = [p.alloc("ws", [128, nk, SW], BF16) for _ in range(2)]
    wsb = [Buf() for _ in range(2)]
    wl = WLoader(p)
    hs = [p.alloc("hs", [128, TB], F32) for _ in range(2)]
    hsb = [Buf() for _ in range(2)]
    ho = [p.alloc("ho", [128, TB], F32) for _ in range(2)]
    hob = [Buf() for _ in range(2)]
    av = None if tiled else actT.rearrange("(k p) t -> p k t", p=128)
    wv = W.rearrange("(k p) n -> p k n", p=128)
    kstep = 4 if nk % 4 == 0 else nk
    atb = [Buf() for _ in range(nk // kstep)]
    NOS = D // SW
    NM = SW // 128
    nslab = (S // TB) * NOS

    def lw(i):
        os_ = i % NOS
        wl.load(ws[i % 2][:, :, :], wv[:, :, os_ * SW:(os_ + 1) * SW], nk, SW, wsb[i % 2])

    def lh(n_):
        tb_i, r = divmod(n_, 16)
        p.dma("sync", hs[n_ % 2][:, :], c.hT[r * 128:(r + 1) * 128, tb_i * TB:(tb_i + 1) * TB], writes=[hsb[n_ % 2]])

    lw(0)
    lh(0)
    n = 0
    for tb_i in range(S // TB):
        tsl = slice(tb_i * TB, (tb_i + 1) * TB)
        for k0 in range(0, nk, kstep):
            srcv = actT[tb_i, :, k0:k0 + kstep, :] if tiled else av[:, k0:k0 + kstep, tsl]
            p.dma("sync", at[:, k0:k0 + kstep, :], srcv, writes=[atb[k0 // kstep]])
        for os_ in range(NOS):
            i = tb_i * NOS + os_
            wsl = i % 2
            if i + 1 < nslab:
                lw(i + 1)
            for m in range(NM):
                ot = os_ * NM + m
                sl = n % 2
                if n + 1 < (S // TB) * 16:
                    lh(n + 1)
                n += 1
                for t in range(TB // 512):
                    b = c.nbk % 8
                    c.nbk += 1
                    for k in range(nk):
                        mm(p, c.ps[:, b, :], ws[wsl][:, k, m * 128:(m + 1) * 128], at[:, k, t * 512:(t + 1) * 512],
                           k == 0, k == nk - 1, reads=[wsb[wsl], atb[k // kstep]], writes=[c.psb[b]],
                           signal=(k == nk - 1 or (k % kstep) == kstep - 1))
                    tt(p, "vector", ho[sl][:, t * 512:(t + 1) * 512], c.ps[:, b, :], hs[sl][:, t * 512:(t + 1) * 512],
                       ALU.add, reads=[c.psb[b], hsb[sl]], writes=[hob[sl]])
                p.dma("scalar", c.hT[ot * 128:(ot + 1) * 128, tsl], ho[sl][:, :], reads=[hob[sl]])
    p.barrier()
    p.sb_off = base


def phase_F1(p, c, l):
    base = p.sb_off
    xn = p.alloc("xn", [128, 16, S], BF16)
    emit_norm(p, c, c.ffng[:, l * 16:(l + 1) * 16], xn, BF16)
    HT = 2048
    ws = [p.alloc("wu", [128, 16, 2, 128], BF16) for _ in range(2)]
    wsb = [Buf() for _ in range(2)]
    U = [p.alloc("U", [128, HT + 2], F32) for _ in range(2)]
    Ub = [Buf() for _ in range(2)]
    C = [p.alloc("C", [128, HT], F32) for _ in range(2)]
    Cb = [Buf() for _ in range(2)]
    A = [p.alloc("A", [128, HT], BF16) for _ in range(1)]
    Ab = [Buf() for _ in range(1)]
    wl = WLoader(p)
    wup = c.W("w_up")[l]
    wview = wup.rearrange("(k p) (g f) -> p k g f", p=128, g=2)
    n = 0
    na = 0

    def lwf(n_):
        fi_ = n_ % NFT
        for g_ in range(2):
            wl.load(ws[n_ % 2][:, :, g_, :], wview[:, :, g_, fi_ * 128:(fi_ + 1) * 128], 16, 128, wsb[n_ % 2])

    for half in range(2):
        t0 = half * HT
        for g in range(2):
            zc = 0 if half == 0 else HT + 1
            p.op("gpsimd", lambda e, o=U[g][:, zc:zc + 1]: e.memset(o, 0.0), writes=[Ub[g]])
        for fi in range(NFT):
            sl = n % 2
            if n == 0:
                lwf(0)
            if n + 1 < 2 * NFT:
                lwf(n + 1)
            n += 1
            cw = lambda g, j: c.convw[:, ((l * 2 + g) * 3 + j) * NFT + fi:((l * 2 + g) * 3 + j) * NFT + fi + 1]
            cb = lambda g: c.convb[:, (l * 2 + g) * NFT + fi:(l * 2 + g) * NFT + fi + 1]
            for g in range(2):
                hcol = HT + 1 if half == 0 else 0
                htok = t0 + HT if half == 0 else t0 - 1
                for k in range(16):
                    mm(p, c.ps[:, 7, 0:1], ws[sl][:, k, g, :], xn[:, k, htok:htok + 1], k == 0, k == 15,
                       reads=[wsb[sl]], writes=[c.psb[7]], signal=(k == 15))
                cp(p, "scalar", U[g][:, hcol:hcol + 1], c.ps[:, 7, 0:1], reads=[c.psb[7]], writes=[Ub[g]])
                for t in range(4):
                    b = c.nbk % 7
                    c.nbk += 1
                    for k in range(16):
                        mm(p, c.ps[:, b, :], ws[sl][:, k, g, :], xn[:, k, t0 + t * 512:t0 + (t + 1) * 512],
                           k == 0, k == 15, reads=[wsb[sl]], writes=[c.psb[b]], signal=(k == 15))
                    cp(p, "scalar", U[g][:, 1 + t * 512:1 + (t + 1) * 512], c.ps[:, b, :], reads=[c.psb[b]],
                       writes=[Ub[g]])
                    act(p, C[g][:, t * 512:(t + 1) * 512], c.ps[:, b, :], AF.Identity, reads=[c.psb[b]],
                        writes=[Cb[g]], scale=cw(g, 1), bias=cb(g))
                stt(p, C[g][:, :], U[g][:, 0:HT], cw(g, 0), C[g][:, :], ALU.mult, ALU.add,
                    reads=[Ub[g], Cb[g]], writes=[Cb[g]])
                stt(p, C[g][:, :], U[g][:, 2:HT + 2], cw(g, 2), C[g][:, :], ALU.mult, ALU.add,
                    reads=[Ub[g], Cb[g]], writes=[Cb[g]])
            act(p, C[0][:, :], C[0][:, :], AF.Silu, reads=[Cb[0]], writes=[Cb[0]])
            asl = 0
            na += 1
            tt(p, "gpsimd", A[asl][:, :], C[0][:, :], C[1][:, :], ALU.mult, reads=[Cb[0], Cb[1]], writes=[Ab[asl]])
            p.dma("gpsimd", c.aT[half * 2:half * 2 + 2, :, fi, :].rearrange("b p t -> p b t"),
                  A[asl][:, :].rearrange("p (b t) -> p b t", b=2), reads=[Ab[asl]])
    p.barrier()
    p.sb_off = base


def phase_B1(p, c, j):
    base = p.sb_off
    xn = p.alloc("xn", [128, 16, S], BF16)
    emit_norm(p, c, c.mixg[:, c.layer * 16:(c.layer + 1) * 16], xn, BF16)
    base2 = p.sb_off
    ost = [p.alloc("ost", [128, S], BF16) for _ in range(2)]
    ostb = [Buf() for _ in range(2)]
    wq = c.W("b_w_qkv")[j]
    c.nproj_banks = 8
    cnt = [0]

    def evac(si, m, t, b):
        g = si // 4
        which = (si % 4) // 2
        h = (si % 2) * 4 + m
        dil = B_GROUPS[g][1]
        L = S // dil
        nj = 512 // dil
        j0 = t * nj
        osl = (si * 4 + m) % 2
        o = ost[osl][:, :].rearrange("p (c j) -> p c j", c=dil)[:, :, j0:j0 + nj]
        i = c.ps[:, b, :].rearrange("p (j c) -> p c j", c=dil)
        cnt[0] += 1
        cp(p, "vector" if (cnt[0] % 2) else "scalar", o, i, reads=[c.psb[b]], writes=[ostb[osl]], merge=True)
        if t == 7:
            row = ((g * 2 + which) * 8 + h) * 128
            p.dma("gpsimd", c.qkb[row:row + 128, :], ost[osl][:, :], reads=[ostb[osl]])

    def wfn(si):
        g = si // 4
        which = (si % 4) // 2
        c0 = g * 3072 + which * 1024 + (si % 2) * 512
        return wq[:, c0:c0 + 512]

    proj_fm(p, c, xn, wfn, 12, 512, evac)
    p.barrier()
    p.sb_off = base2

    def wfv(si):
        g = si // 2
        c0 = g * 3072 + 2048 + (si % 2) * 512
        return wq[:, c0:c0 + 512]

    def store(si, t4, st, stb):
        dst = c.vb[t4 * 512:(t4 + 1) * 512, si * 512:(si + 1) * 512].rearrange("(q p) n -> p q n", p=128)
        p.dma("gpsimd", dst, st[:, :, :], reads=[stb])

    proj_tm(p, c, xn, wfv, 6, store)
    p.barrier()
    p.sb_off = base


def phase_B2(p, c):
    base = p.sb_off
    scale = float(HD) ** -0.5
    Zt = p.alloc("Zt", [128, S], F32)
    Ztb = Buf()
    Ug = [p.alloc("Ug", [128, S], F32) for _ in range(3)]
    Ugb = [Buf() for _ in range(3)]
    bt = [p.alloc("bt", [128, 3, 256], F32) for _ in range(2)]
    btb = [Buf() for _ in range(2)]
    qs = [p.alloc("qs", [128, S], BF16) for _ in range(2)]
    kp = [p.alloc("kp", [128, S + 16 * 128], BF16) for _ in range(2)]
    vp = [p.alloc("vp", [128, 48 * 128], BF16) for _ in range(2)]
    inb = [Buf() for _ in range(2)]
    NL = 6
    lg = [p.alloc("lg", [128, 256], F32) for _ in range(NL)]
    lgb = [Buf() for _ in range(NL)]
    pT = [p.alloc("pT", [128, 256], BF16) for _ in range(NL)]
    pTb = [Buf() for _ in range(NL)]
    yst = [p.alloc("yst", [128, S], BF16) for _ in range(2)]
    ystb = [Buf() for _ in range(2)]
    nin = 0
    nsb = 0
    nl = 0
    nob = 0
    ny = 0
    def load_in(idx):
        hh, g = divmod(idx, 3)
        dil = B_GROUPS[g][1]
        L = S // dil
        nblk = L // 128
        nch = nblk + 1
        sl = idx % 2
        ib = inb[sl]
        MG = dict(merge=True)
        p.dma("sync", bt[sl][:, :, :], c.biasT[hh, g], writes=[btb[sl]])
        rq = ((g * 2 + 0) * 8 + hh) * 128
        rk = ((g * 2 + 1) * 8 + hh) * 128
        p.dma("sync", qs[sl][:, :], c.qkb[rq:rq + 128, :], writes=[ib], **MG)
        kpv = kp[sl][:, 0:dil * (L + 128)].rearrange("p (c j) -> p c j", c=dil)
        p.op("gpsimd", lambda e, o=kpv[:, :, 0:64]: e.memset(o, 0.0), writes=[ib], **MG)
        p.op("gpsimd", lambda e, o=kpv[:, :, 64 + L:128 + L]: e.memset(o, 0.0), writes=[ib], **MG)
        p.dma("sync", kpv[:, :, 64:64 + L], c.qkb[rk:rk + 128, :].rearrange("p (c j) -> p c j", c=dil),
              writes=[ib], **MG)
        vpv = vp[sl][:, 0:dil * nch * 128].rearrange("p (c i d) -> p c i d", c=dil, i=nch)
        col = (g * 8 + hh) * 128
        vv = c.vb[:, col:col + 128].rearrange("(j c) d -> j c d", c=dil)
        p.op("gpsimd", lambda e, o=vpv[0:64, :, 0, :]: e.memset(o, 0.0), writes=[ib], **MG)
        p.op("gpsimd", lambda e, o=vpv[64:128, :, nch - 1, :]: e.memset(o, 0.0), writes=[ib], **MG)
        p.dma("sync", vpv[64:128, :, 0, :], vv[0:64, :, :], writes=[ib], **MG)
        p.dma("sync", vpv[0:64, :, nch - 1, :], vv[L - 64:L, :, :], writes=[ib], **MG)
        if nch - 2 == 1:
            p.dma("sync", vpv[:, :, 1, :], vv[64:L - 64, :, :], writes=[ib], **MG)
        else:
            for cc in range(dil):
                src = vv[64:L - 64, cc, :].rearrange("(i p) d -> p i d", p=128)
                p.dma("sync", vpv[:, cc, 1:nch - 1, :], src, writes=[ib], **MG)

    load_in(0)
    for hh in range(8):
        for g in range(3):
            dil = B_GROUPS[g][1]
            L = S // dil
            nblk = L // 128
            nch = nblk + 1
            sl = nin % 2
            nin += 1
            ib = inb[sl]
            if nin < 24:
                load_in(nin)
            kpv = kp[sl][:, 0:dil * (L + 128)].rearrange("p (c j) -> p c j", c=dil)
            vpv = vp[sl][:, 0:dil * nch * 128].rearrange("p (c i d) -> p c i d", c=dil, i=nch)
            Uv = Ug[g][:, :].rearrange("p (j c) -> p c j", c=dil)
            Zv = Zt[:, :].rearrange("p (j c) -> p c j", c=dil)
            nb4 = min(4, nblk)
            items = []
            for cc in range(dil):
                for b4 in range(nblk // nb4):
                    for qi in range(nb4):
                        items.append((cc, b4, qi))
            sbank = {}

            def qk(ii):
                nonlocal nsb
                cc, b4, qi = items[ii]
                b = b4 * nb4 + qi
                bs = nsb % 4
                nsb += 1
                sbank[ii] = bs
                qap = qs[sl][:, cc * L + b * 128:cc * L + (b + 1) * 128]
                mm(p, c.ps[:, bs, 0:128], kpv[:, cc, b * 128:(b + 1) * 128], qap, True, True,
                   reads=[ib], writes=[c.psb[bs]], signal=False)
                mm(p, c.ps[:, bs, 128:256], kpv[:, cc, (b + 1) * 128:(b + 2) * 128], qap, True, True,
                   reads=[ib], writes=[c.psb[bs]], signal=True)

            wslot = {}

            def front(ii):
                nonlocal nl
                cc, b4, qi = items[ii]
                b = b4 * nb4 + qi
                var = 1 if b == 0 else (2 if b == nblk - 1 else 0)
                bs = sbank.pop(ii)
                w = nl % NL
                nl += 1
                wslot[ii] = w
                stt(p, lg[w][:, :], c.ps[:, bs, 0:256], scale, bt[sl][:, var, :], ALU.mult, ALU.add,
                    reads=[c.psb[bs], btb[sl]], writes=[lgb[w]])
                act(p, pT[w][:, :], lg[w][:, :], AF.Exp, reads=[lgb[w]], writes=[pTb[w]])

            def back(ii):
                nonlocal nob
                cc, b4, qi = items[ii]
                b = b4 * nb4 + qi
                w = wslot.pop(ii)
                if qi == 0:
                    nob += 1
                bo = 4 + (nob % 2)
                bz = 6 + (nob % 2)
                osl = slice(qi * 128, (qi + 1) * 128)
                mm(p, c.ps[:, bo, osl], vpv[:, cc, b, :], pT[w][:, 0:128], True, False,
                   reads=[pTb[w], ib], writes=[c.psb[bo]], signal=False)
                mm(p, c.ps[:, bo, osl], vpv[:, cc, b + 1, :], pT[w][:, 128:256], False, True,
                   reads=[pTb[w], ib], writes=[c.psb[bo]], signal=True)
                mm(p, c.ps[:, bz, osl], c.ones_b[:, :], pT[w][:, 0:128], True, False,
                   reads=[pTb[w]], writes=[c.psb[bz]], signal=False)
                mm(p, c.ps[:, bz, osl], c.ones_b[:, :], pT[w][:, 128:256], False, True,
                   reads=[pTb[w]], writes=[c.psb[bz]], signal=True)
                if qi == nb4 - 1:
                    j0 = b4 * nb4 * 128
                    nj = nb4 * 128
                    cp(p, "scalar", Uv[:, cc, j0:j0 + nj], c.ps[:, bo, 0:nj], reads=[c.psb[bo]], writes=[Ugb[g]])
                    if g == 0:
                        cp(p, "vector", Zv[:, cc, j0:j0 + nj], c.ps[:, bz, 0:nj], reads=[c.psb[bz]], writes=[Ztb])
                    else:
                        tt(p, "vector", Zv[:, cc, j0:j0 + nj], c.ps[:, bz, 0:nj], Zv[:, cc, j0:j0 + nj], ALU.add,
                           reads=[c.psb[bz], Ztb], writes=[Ztb])

            LA = 4
            LF = 2
            nit = len(items)
            for ii in range(min(LA, nit)):
                qk(ii)
            for ii in range(min(LF, nit)):
                front(ii)
            for ii in range(nit):
                if ii + LA < nit:
                    qk(ii + LA)
                if ii + LF < nit:
                    front(ii + LF)
                back(ii)
        p.op("vector", lambda e, o=Zt[:, :]: e.reciprocal(out=o, in_=o), reads=[Ztb], writes=[Ztb])
        for g in range(3):
            ysl = ny % 2
            ny += 1
            tt(p, "vector" if g != 1 else "gpsimd", yst[ysl][:, :], Ug[g][:, :], Zt[:, :], ALU.mult,
               reads=[Ugb[g], Ztb], writes=[ystb[ysl]])
            row = (g * 8 + hh) * 128
            p.dma("gpsimd", c.yT[row:row + 128, :], yst[ysl][:, :], reads=[ystb[ysl]])
    p.barrier()
    p.sb_off = base


def phase_final(p, c):
    base = p.sb_off
    hs = [p.alloc("hs", [128, 16, 512], F32) for _ in range(2)]
    hsb = [Buf() for _ in range(2)]
    sq = [p.alloc("sq", [128, 16, 512], BF16) for _ in range(2)]
    sqb = [Buf() for _ in range(2)]
    rs = [p.alloc("rs", [128, 512], F32) for _ in range(2)]
    rsb = [Buf() for _ in range(2)]
    xs = p.alloc("xs", [128, 16, 512], F32)
    xsb = Buf()
    ot = [p.alloc("ot", [128, D], F32) for _ in range(2)]
    otb = [Buf() for _ in range(2)]
    hTv = c.hT.rearrange("(c p) t -> p c t", p=128)
    no = 0
    for j in range(8):
        sl = j % 2
        p.dma("sync", hs[sl][:, :, :], hTv[:, :, j * 512:(j + 1) * 512], writes=[hsb[sl]])
        b = c.nbk % 8
        c.nbk += 1
        act(p, sq[sl][:, :, :], hs[sl][:, :, :], AF.Square, reads=[hsb[sl]], writes=[sqb[sl]])
        for k in range(16):
            mm(p, c.ps[:, b, :], c.ones_db[:, :], sq[sl][:, k, :], k == 0, k == 15,
               reads=[sqb[sl]], writes=[c.psb[b]], signal=(k == 15))
        act(p, rs[sl][:, :], c.ps[:, b, :], AF.Sqrt, reads=[c.psb[b]], writes=[rsb[sl]], bias=c.eps_t[:, 0:1])
        p.op("vector", lambda e, o=rs[sl][:, :]: e.reciprocal(out=o, in_=o), reads=[rsb[sl]], writes=[rsb[sl]])
        for k in range(16):
            stt(p, xs[:, k, :], hs[sl][:, k, :], c.fing[:, k:k + 1], rs[sl][:, :],
                ALU.mult, ALU.mult, reads=[hsb[sl], rsb[sl]], writes=[xsb])
        for sub in range(4):
            osl = no % 2
            no += 1
            for c4 in range(4):
                b = c.nbk % 8
                c.nbk += 1
                for jj in range(4):
                    k = c4 * 4 + jj
                    o = c.ps[:, b, jj * 128:(jj + 1) * 128]
                    i = xs[:, k, sub * 128:(sub + 1) * 128]
                    p.op("tensor", lambda e, o=o, i=i: e.transpose(o, i, c.ident_f[:, :]),
                         reads=[xsb], writes=[c.psb[b]], signal=(jj == 3))
                cp(p, "vector" if (c4 % 2) else "scalar", ot[osl][:, c4 * 512:(c4 + 1) * 512], c.ps[:, b, :],
                   reads=[c.psb[b]], writes=[otb[osl]], merge=True)
            t = j * 4 + sub
            p.dma("gpsimd", c.out[t * 128:(t + 1) * 128, :], ot[osl][:, :], reads=[otb[osl]])
    p.barrier()
    p.sb_off = base


PHASES_ALL = None


def build(stop_after=None, dbg=()):
    nc = bass.Bass("TRN2", target_bir_lowering=False)
    es = ExitStack()
    c = Ctx()

    def din(name, shape, dt=F32):
        return nc.dram_tensor(name, list(shape), dt, kind="ExternalInput").ap()

    def dscr(name, shape, dt):
        return nc.dram_tensor(name, list(shape), dt, kind=("ExternalOutput" if (dbg and name in dbg) else "Internal")).ap()

    c.x = din("x", [S, D])
    wshapes = {"a_w_qkv": [2, D, 3072], "a_w_o": [2, D, D], "b_w_qkv": [2, D, 9216], "b_w_o": [2, 3072, D],
               "w_up": [4, D, 2 * DFF], "w_down": [4, DFF, D]}
    wcache = {}

    def W(name):
        if name not in wcache:
            wcache[name] = din(name, wshapes[name])
        return wcache[name]

    c.W = W
    import os
    if os.environ.get("MK_ALLW"):
        for k_ in wshapes:
            W(k_)
    smalls = {
        "mixg": 64, "ffng": 64, "fing": 16, "aqg": 2, "akg": 2, "convw": 4 * 2 * 3 * NFT, "convb": 4 * 2 * NFT,
        "ident_f": 128, "ones_f": 128, "ones_d": 128, "ones_h": 128, "perm": 128, "eps_t": 1,
    }
    sd = {k: din(k + "_d", [128, n]) for k, n in smalls.items()}
    ones_b_d = din("ones_b_d", [128, 128], BF16)
    c.cos_d = din("cos_d", [128, S])
    c.sin_d = din("sin_d", [128, S])
    c.biasT = din("biasT", [8, 3, 128, 3, 256])
    c.out = nc.dram_tensor("out", [S, D], F32, kind="ExternalOutput").ap()
    c.hT = dscr("hT", [D, S], F32)
    c.qT = dscr("qT", [2048, S], BF16)
    c.kT = dscr("kT", [512, S], BF16)
    c.v = dscr("v", [S, 512], BF16)
    c.aoT = dscr("aoT", [2048, S], BF16)
    c.aT = dscr("aT", [4, 128, NFT, 1024], BF16)
    c.qkb = dscr("qkb", [3 * 2 * 8 * 128, S], BF16)
    c.vb = dscr("vb", [S, 3072], BF16)
    c.yT = dscr("yT", [3072, S], BF16)
    c.dbgz = dscr("dbgz", [128, 1028], F32) if (dbg and "dbgz" in dbg) else None

    p = Prog(nc, es)
    c.ps = nc.alloc_psum_tensor("ps", [128, 8, 512], F32)
    c.psb = [Buf() for _ in range(8)]
    c.nbk = 0
    c.nproj_banks = 8
    cb = Buf()
    for k, n in smalls.items():
        t = p.alloc(k, [128, n], F32)
        setattr(c, k, t)
        p.dma("sync", t[:, :], sd[k], writes=[cb])
    c.ones_b = p.alloc("ones_b", [128, 128], BF16)
    p.dma("sync", c.ones_b[:, :], ones_b_d, writes=[cb])
    c.ones_db = p.alloc("ones_db", [128, 128], BF16)
    c.ones_hb = p.alloc("ones_hb", [128, 128], BF16)
    cb2 = Buf()
    p.op("vector", lambda e: e.tensor_copy(out=c.ones_db[:, :], in_=c.ones_d[:, :]), reads=[cb], writes=[cb2])
    p.op("vector", lambda e: e.tensor_copy(out=c.ones_hb[:, :], in_=c.ones_h[:, :]), reads=[cb], writes=[cb2])
    p.barrier()

    def run():
        phase_x_to_hT(p, c)
        if stop_after == "T0":
            return
        for l in range(DEPTH):
            c.layer = l
            j = l // 2
            if l % 2 == 0:
                phase_A1(p, c, j)
                if stop_after == f"A1_{l}":
                    return
                phase_A2(p, c)
                if stop_after == f"A2_{l}":
                    return
                phase_proj_res(p, c, c.aoT, c.W("a_w_o")[j], 16, 2048)
            else:
                phase_B1(p, c, j)
                if stop_after == f"B1_{l}":
                    return
                phase_B2(p, c)
                if stop_after == f"B2_{l}":
                    return
                phase_proj_res(p, c, c.yT, c.W("b_w_o")[j], 24, 2048)
            if stop_after == f"M_{l}":
                return
            phase_F1(p, c, l)
            if stop_after == f"F1_{l}":
                return
            phase_proj_res(p, c, c.aT, c.W("w_down")[l], NFT, 1024, SW=256, tiled=True)
            if stop_after == f"F_{l}":
                return
        phase_final(p, c)

    run()
    with nc.Block() as block:
        p.flush(block)
    es.close()
    p.used_inputs = set(wcache)
    return nc, p


def _t5_bucket(rel):
    nb = 16
    max_exact = 8
    base = np.where(rel > 0, nb, 0)
    n = np.abs(rel)
    nf = np.maximum(n, 1).astype(np.float32)
    large = max_exact + (np.log(nf / np.float32(max_exact)) / np.float32(np.log(1024 / max_exact))
                         * (nb - max_exact)).astype(np.int32)
    large = np.minimum(large, nb - 1)
    return base + np.where(n < max_exact, n, large)


def _consts():
    half, quarter = 64, 32
    inv_freq = (np.float32(10000.0) ** (-np.arange(quarter, dtype=np.float32) / np.float32(quarter))).astype(np.float32)
    t = np.arange(S)
    row = (t // 64).astype(np.float32)
    colp = (t % 64).astype(np.float32)
    cosT = np.zeros((128, S), np.float32)
    sinT = np.zeros((128, S), np.float32)
    perm = np.zeros((128, 128), np.float32)
    for d in range(128):
        pos = row if d < 64 else colp
        i = d % 64
        fi = i % 32
        ang = (pos * inv_freq[fi]).astype(np.float32)
        cosT[d] = np.cos(ang)
        first = i < 32
        sinT[d] = (-np.sin(ang) if first else np.sin(ang))
        pd = d + 32 if first else d - 32
        perm[pd, d] = 1.0
    return cosT, sinT, perm


def _chunk_cols(v, n):
    return v


def kernel(x, a_w_qkv, a_w_o, a_q_gain, a_k_gain, b_w_qkv, b_w_o, rel_bias,
           mix_norm, ffn_norm, w_up, conv_w, conv_b, w_down, final_norm, _stop_after=None, _dbg=(), _ncores=8):
    f32 = np.float32
    x = np.asarray(x, f32)
    nb = x.shape[0]
    cosT, sinT, perm = _consts()

    def pc(v, n):
        v = np.asarray(v, f32)
        lead = v.shape[:-1]
        r = v.reshape(lead + (n, 128))
        r = np.moveaxis(r, -1, 0)
        return np.ascontiguousarray(r.reshape(128, -1))

    smalls = {
        "mixg_d": pc(mix_norm, 16), "ffng_d": pc(ffn_norm, 16), "fing_d": pc(final_norm, 16),
        "aqg_d": np.ascontiguousarray(np.asarray(a_q_gain, f32).T), "akg_d": np.ascontiguousarray(np.asarray(a_k_gain, f32).T),
        "convw_d": np.ascontiguousarray(np.asarray(conv_w, f32).reshape(4, 3, 2, NFT, 128).transpose(4, 0, 2, 1, 3).reshape(128, -1)),
        "convb_d": np.ascontiguousarray(np.asarray(conv_b, f32).reshape(4, 2, NFT, 128).transpose(3, 0, 1, 2).reshape(128, -1)),
        "ident_f_d": np.eye(128, dtype=f32), "ones_f_d": np.ones((128, 128), f32), "ones_d_d": np.full((128, 128), 1.0 / D, f32),
        "ones_h_d": np.full((128, 128), 1.0 / HD, f32), "perm_d": perm, "eps_t_d": np.full((128, 1), EPS, f32),
        "ones_b_d": np.ones((128, 128), ml_dtypes.bfloat16),
        "cos_d": cosT, "sin_d": sinT,
    }
    rb = np.asarray(rel_bias, f32)
    pp = np.arange(128)[:, None]
    ff = np.arange(128)[None, :]
    biasT = np.empty((8, 3, 128, 3, 256), f32)
    for g, (_, dil) in enumerate(B_GROUPS):
        relA = pp - 64 - ff
        relB = pp + 64 - ff
        okA = np.abs(relA) <= 64
        okB = np.abs(relB) <= 64
        bkA = _t5_bucket(relA * dil)
        bkB = _t5_bucket(relB * dil)
        for hh in range(8):
            col = rb[:, g * 8 + hh]
            A = np.where(okA, col[bkA], f32(NEG))
            Bm = np.where(okB, col[bkB], f32(NEG))
            Ae = np.where(okA & (pp >= 64), col[bkA], f32(NEG))
            Be = np.where(okB & (pp < 64), col[bkB], f32(NEG))
            biasT[hh, g, :, 0, :128] = A
            biasT[hh, g, :, 0, 128:] = Bm
            biasT[hh, g, :, 1, :128] = Ae
            biasT[hh, g, :, 1, 128:] = Bm
            biasT[hh, g, :, 2, :128] = A
            biasT[hh, g, :, 2, 128:] = Be
    shared = dict(smalls)
    shared.update({
        "a_w_qkv": np.asarray(a_w_qkv, f32), "a_w_o": np.asarray(a_w_o, f32), "b_w_qkv": np.asarray(b_w_qkv, f32),
        "b_w_o": np.asarray(b_w_o, f32), "w_up": np.asarray(w_up, f32), "w_down": np.asarray(w_down, f32),
        "biasT": biasT,
    })
    nc, p = build(stop_after=_stop_after, dbg=_dbg)
    ncores = _ncores
    in_maps = []
    for k in list(shared):
        if k in ("a_w_qkv", "a_w_o", "b_w_qkv", "b_w_o", "w_up", "w_down") and k not in p.used_inputs:
            del shared[k]
    for i in range(ncores):
        m = dict(shared)
        m["x"] = np.ascontiguousarray(x[i % nb])
        in_maps.append(m)
    res = run_bass_kernel_spmd(nc, in_maps, core_ids=list(range(ncores)))
    if _dbg:
        return res
    return np.stack([res.results[i]["out"] for i in range(ncores)], axis=0).astype(f32)
```
